# Optimizing a Trainium2 kernel written in Bass

```python
import jax, jax.numpy as jnp
from jax import lax
import numpy as np

D_MODEL = 1024
BATCH = 32
SEQ = 2048
DEPTH = 4

ROPE_BASE = 10000.0
NORM_EPS = 1e-6
MLA_HEADS = D_MODEL // 128
MLA_QK_NOPE = 128
MLA_QK_ROPE = 64
MLA_V_DIM = 128
MLA_Q_LORA = 3 * D_MODEL // 8
MLA_KV_LORA = D_MODEL // 8
ATTN_Q_BLOCK = 128
POOL_WINDOWS = (2, 4, 8, 16)
POOL_GROUPS = len(POOL_WINDOWS)
POOL_GROUP_DIM = D_MODEL // POOL_GROUPS
MIX_A = MLA_HEADS * MLA_V_DIM
MIX_B = POOL_GROUPS * POOL_GROUP_DIM
EVEN_MIX = MIX_A + MIX_B
EVEN_SPLITS = (MLA_Q_LORA, MLA_Q_LORA + MLA_KV_LORA, MLA_Q_LORA + MLA_KV_LORA + MLA_QK_ROPE,
               MLA_Q_LORA + MLA_KV_LORA + MLA_QK_ROPE + MIX_B)
EVEN_IN = EVEN_SPLITS[-1] + EVEN_MIX
RET_HEADS = D_MODEL // 256
RET_DK = 256
RET_DV = 512
RET_CHUNK = 128
RET_QK = RET_HEADS * RET_DK
RET_V = RET_HEADS * RET_DV
ODD_IN = 2 * RET_QK + 2 * RET_V
N_EVEN = (DEPTH + 1) // 2
N_ODD = DEPTH // 2

kernel_name = 'hybrid_mla_pool_retention_encoder'


def rms_norm(x, g):
    xf = x.astype(jnp.float32)
    y = xf * lax.rsqrt(jnp.mean(xf * xf, axis=-1, keepdims=True) + NORM_EPS)
    return (y * g.astype(jnp.float32)).astype(x.dtype)


def rope_tables(positions, dim):
    inv = 1.0 / (ROPE_BASE ** (jnp.arange(0, dim, 2, dtype=jnp.float32) / dim))
    ang = positions.astype(jnp.float32)[..., None] * inv
    return jnp.cos(ang)[:, :, None, :], jnp.sin(ang)[:, :, None, :]


def apply_rope(x, cos, sin):
    half = x.shape[-1] // 2
    xf = x.astype(jnp.float32)
    x1, x2 = xf[..., :half], xf[..., half:]
    return jnp.concatenate([x1 * cos - x2 * sin, x2 * cos + x1 * sin], axis=-1).astype(x.dtype)


def mla(cq, ckv, kr, cos, sin, q_norm_g, w_uq, kv_norm_g, w_ukv):
    B_, T, _ = cq.shape
    dqk = MLA_QK_NOPE + MLA_QK_ROPE
    q = (rms_norm(cq, q_norm_g) @ w_uq).reshape(B_, T, MLA_HEADS, dqk)
    q = jnp.concatenate([q[..., :MLA_QK_NOPE], apply_rope(q[..., MLA_QK_NOPE:], cos, sin)], axis=-1)
    kv = (rms_norm(ckv, kv_norm_g) @ w_ukv).reshape(B_, T, MLA_HEADS, MLA_QK_NOPE + MLA_V_DIM)
    k_rope = apply_rope(kr[:, :, None, :], cos, sin)
    k = jnp.concatenate([kv[..., :MLA_QK_NOPE],
                         jnp.broadcast_to(k_rope, (B_, T, MLA_HEADS, MLA_QK_ROPE))], axis=-1)
    v = kv[..., MLA_QK_NOPE:]
    nb = T // ATTN_Q_BLOCK
    qb = q.reshape(B_, nb, ATTN_Q_BLOCK, MLA_HEADS, dqk).transpose(1, 0, 2, 3, 4)
    scale = dqk ** -0.5

    def attend(q_blk):
        s = jnp.einsum('bqhd,bkhd->bhqk', q_blk, k, preferred_element_type=jnp.float32) * scale
        p = jax.nn.softmax(s, axis=-1).astype(v.dtype)
        return jnp.einsum('bhqk,bkhd->bqhd', p, v)

    o = lax.map(attend, qb)
    return o.transpose(1, 0, 2, 3, 4).reshape(B_, T, MIX_A)


def multiscale_pool(u, pool_w, pool_scale):
    B_, T, _ = u.shape
    uf = u.reshape(B_, T, POOL_GROUPS, POOL_GROUP_DIM).astype(jnp.float32)
    cs = jnp.concatenate([jnp.zeros_like(uf[:, :1]), jnp.cumsum(uf, axis=1)], axis=1)
    t = jnp.arange(T)
    outs = []
    for g, w in enumerate(POOL_WINDOWS):
        left = w // 2
        right = w - 1 - left
        lo = jnp.clip(t - left, 0, T - 1)
        hi = jnp.clip(t + right, 0, T - 1)
        csg = cs[:, :, g]
        s = jnp.take(csg, hi + 1, axis=1) - jnp.take(csg, lo, axis=1)
        cnt = (hi - lo + 1).astype(jnp.float32)
        outs.append(s / cnt[None, :, None] - uf[:, :, g])
    d = jnp.stack(outs, axis=2).astype(u.dtype)
    y = jnp.einsum('btgc,gcd->btgd', d, pool_w) * pool_scale.reshape(POOL_GROUPS, POOL_GROUP_DIM)
    return y.reshape(B_, T, MIX_B)


def hybrid_attn_pool_layer(x, cos_a, sin_a, norm_g, w_in, q_norm_g, w_uq, kv_norm_g, w_ukv,
                           pool_w, pool_scale, w_out):
    h = rms_norm(x, norm_g)
    proj = h @ w_in
    cq, ckv, kr, u, gate = jnp.split(proj, list(EVEN_SPLITS), axis=-1)
    a = mla(cq, ckv, kr, cos_a, sin_a, q_norm_g, w_uq, kv_norm_g, w_ukv)
    b = multiscale_pool(u, pool_w, pool_scale)
    y = jnp.concatenate([a, b], axis=-1) * jax.nn.silu(gate)
    return y @ w_out


def retention_chunkwise(q, k, v, log_g, include_diag):
    B_, T, H, dk = q.shape
    dv = v.shape[-1]
    C = RET_CHUNK
    N = T // C
    to_chunks = lambda a: a.reshape(B_, N, C, H, a.shape[-1]).transpose(1, 0, 3, 2, 4)
    qc, kc, vc = to_chunks(q), to_chunks(k), to_chunks(v)
    idx = jnp.arange(C).astype(jnp.float32)
    diff = idx[:, None] - idx[None, :]
    mask = (diff >= 0) if include_diag else (diff > 0)
    dmat = jnp.where(mask[None], jnp.exp(jnp.where(mask, diff, 0.0)[None] * log_g[:, None, None]), 0.0)
    xi = jnp.exp((idx + 1.0)[None, :] * log_g[:, None])[None, :, :, None]
    zeta = jnp.exp((C - 1.0 - idx)[None, :] * log_g[:, None])[None, :, :, None]
    chunk_decay = jnp.exp(C * log_g)[None, :, None, None]

    def step(state, inp):
        q_i, k_i, v_i = inp
        scores = jnp.einsum('bhid,bhjd->bhij', q_i, k_i) * dmat
        o_in = jnp.einsum('bhij,bhjv->bhiv', scores, v_i)
        o_cross = jnp.einsum('bhid,bhdv->bhiv', q_i, state) * xi
        state = state * chunk_decay + jnp.einsum('bhjd,bhjv->bhdv', k_i * zeta, v_i)
        return state, o_in + o_cross

    s0 = jnp.zeros((B_, H, dk, dv), jnp.float32)
    _, o = lax.scan(step, s0, (qc, kc, vc))
    return o.transpose(1, 0, 3, 2, 4).reshape(B_, T, H, dv)


def retention_layer(x, cos_r, sin_r, norm_g, w_in, decay_fwd, decay_bwd, gn_g, w_out):
    B_, T, _ = x.shape
    h = rms_norm(x, norm_g)
    proj = h @ w_in
    q, k, v, gate = jnp.split(proj, [RET_QK, 2 * RET_QK, 2 * RET_QK + RET_V], axis=-1)
    q = apply_rope(q.reshape(B_, T, RET_HEADS, RET_DK), cos_r, sin_r).astype(jnp.float32)
    k = apply_rope(k.reshape(B_, T, RET_HEADS, RET_DK), cos_r, sin_r).astype(jnp.float32) * (RET_DK ** -0.5)
    v = v.reshape(B_, T, RET_HEADS, RET_DV).astype(jnp.float32)
    lf = jax.nn.log_sigmoid(decay_fwd.astype(jnp.float32))
    lb = jax.nn.log_sigmoid(decay_bwd.astype(jnp.float32))
    o_f = retention_chunkwise(q, k, v, lf, True)
    o_b = jnp.flip(retention_chunkwise(jnp.flip(q, 1), jnp.flip(k, 1), jnp.flip(v, 1), lb, False), 1)
    o = o_f + o_b
    mu = jnp.mean(o, axis=-1, keepdims=True)
    var = jnp.mean(jnp.square(o - mu), axis=-1, keepdims=True)
    o = ((o - mu) * lax.rsqrt(var + NORM_EPS)).reshape(B_, T, RET_V) * gn_g.astype(jnp.float32)
    y = o.astype(x.dtype) * jax.nn.silu(gate)
    return y @ w_out


def setup_inputs(seed: int = 0) -> dict:
    key = jax.random.key(seed)
    ks = jax.random.split(key, 24)
    f32 = jnp.float32
    nrm = lambda k, shape, scale: jax.random.normal(k, shape, f32) * scale
    gain = lambda k, shape: 1.0 + 0.05 * jax.random.normal(k, shape, f32)
    x = jax.random.normal(ks[0], (BATCH, SEQ, D_MODEL), f32)
    positions = (jnp.arange(SEQ, dtype=jnp.int32)[None, :]
                 + jax.random.randint(ks[1], (BATCH, 1), 0, 4096, dtype=jnp.int32))
    base_decay = jnp.log(2.0 ** (5.0 + jnp.arange(RET_HEADS, dtype=f32)) - 1.0)
    return {
        'x': x,
        'positions': positions,
        'a_norm_g': gain(ks[2], (N_EVEN, D_MODEL)),
        'a_w_in': nrm(ks[3], (N_EVEN, D_MODEL, EVEN_IN), D_MODEL ** -0.5),
        'a_q_norm_g': gain(ks[4], (N_EVEN, MLA_Q_LORA)),
        'a_w_uq': nrm(ks[5], (N_EVEN, MLA_Q_LORA, MLA_HEADS * (MLA_QK_NOPE + MLA_QK_ROPE)), MLA_Q_LORA ** -0.5),
        'a_kv_norm_g': gain(ks[6], (N_EVEN, MLA_KV_LORA)),
        'a_w_ukv': nrm(ks[7], (N_EVEN, MLA_KV_LORA, MLA_HEADS * (MLA_QK_NOPE + MLA_V_DIM)), MLA_KV_LORA ** -0.5),
        'a_pool_w': nrm(ks[8], (N_EVEN, POOL_GROUPS, POOL_GROUP_DIM, POOL_GROUP_DIM), POOL_GROUP_DIM ** -0.5),
        'a_pool_scale': 1.0 + 0.1 * jax.random.normal(ks[9], (N_EVEN, MIX_B), f32),
        'a_w_out': nrm(ks[10], (N_EVEN, EVEN_MIX, D_MODEL), EVEN_MIX ** -0.5),
        'r_norm_g': gain(ks[11], (N_ODD, D_MODEL)),
        'r_w_in': nrm(ks[12], (N_ODD, D_MODEL, ODD_IN), D_MODEL ** -0.5),
        'r_decay_fwd': base_decay[None, :] + 0.1 * jax.random.normal(ks[13], (N_ODD, RET_HEADS), f32),
        'r_decay_bwd': base_decay[None, :] + 0.1 * jax.random.normal(ks[14], (N_ODD, RET_HEADS), f32),
        'r_gn_g': gain(ks[15], (N_ODD, RET_V)),
        'r_w_out': nrm(ks[16], (N_ODD, RET_V, D_MODEL), RET_V ** -0.5),
        'final_norm_g': gain(ks[17], (D_MODEL,)),
    }


def reference(x, positions, a_norm_g, a_w_in, a_q_norm_g, a_w_uq, a_kv_norm_g, a_w_ukv,
              a_pool_w, a_pool_scale, a_w_out, r_norm_g, r_w_in, r_decay_fwd, r_decay_bwd,
              r_gn_g, r_w_out, final_norm_g):
    cos_a, sin_a = rope_tables(positions, MLA_QK_ROPE)
    cos_r, sin_r = rope_tables(positions, RET_DK)
    for layer in range(DEPTH):
        i = layer // 2
        if layer % 2 == 0:
            x = x + hybrid_attn_pool_layer(x, cos_a, sin_a, a_norm_g[i], a_w_in[i], a_q_norm_g[i],
                                           a_w_uq[i], a_kv_norm_g[i], a_w_ukv[i], a_pool_w[i],
                                           a_pool_scale[i], a_w_out[i])
        else:
            x = x + retention_layer(x, cos_r, sin_r, r_norm_g[i], r_w_in[i], r_decay_fwd[i],
                                    r_decay_bwd[i], r_gn_g[i], r_w_out[i])
    return rms_norm(x, final_norm_g)
```

```python
import numpy as np
import concourse.bass as bass
import concourse.mybir as mybir
from concourse.bass_utils import run_bass_kernel_spmd

F32 = mybir.dt.float32
BF16 = mybir.dt.bfloat16
I32 = mybir.dt.int32
AF = mybir.ActivationFunctionType
ALU = mybir.AluOpType

D = 1024
KC = 8
EPS = 1e-6
N_CORES = 8


class Tok:
    __slots__ = ("lastw", "readers", "name")

    def __init__(self, name=""):
        self.lastw = None
        self.readers = {}
        self.name = name


class _Op:
    __slots__ = ("fn", "waits", "signal", "dkey")


ENGS = ("pe", "act", "dve", "pool", "sp")


class _Rec:
    def __getattr__(self, name):
        def f(*a, **k):
            self.__dict__["call"] = (name, a, k)
            return None
        return f


class Sched:
    def __init__(self, nc):
        self.nc = nc
        self.streams = {e: [] for e in ENGS}
        self.waited = {e: {} for e in ENGS}
        self.dcount = {}
        self.dorder = []

    def _deps(self, eng, reads, writes):
        deps = {}

        def add(k, v):
            if deps.get(k, -1) < v:
                deps[k] = v

        for t in reads:
            if t.lastw is not None:
                add(*t.lastw)
        for t in writes:
            if t.lastw is not None:
                add(*t.lastw)
            for k, v in t.readers.items():
                add(k, v)
        if eng == "pe":
            deps.pop(("e", "pe"), None)
        w = self.waited[eng]
        out = []
        for k, v in deps.items():
            if w.get(k, -1) >= v:
                continue
            w[k] = v
            out.append((k, v))
            if k[0] == "e":
                self.streams[k[1]][v].signal = True
        return out

    def _mark(self, ref, reads, writes):
        k, v = ref
        ws = set(id(t) for t in writes)
        for t in writes:
            t.lastw = ref
            t.readers = {}
        for t in reads:
            if id(t) not in ws:
                if t.readers.get(k, -1) < v:
                    t.readers[k] = v

    def op(self, eng, fn, reads=(), writes=()):
        o = _Op()
        rec = _Rec()
        fn(rec)
        name, a, k = rec.call
        o.fn = lambda e, name=name, a=a, k=k: getattr(e, name)(*a, **k)
        o.signal = False
        o.dkey = None
        o.waits = self._deps(eng, reads, writes)
        idx = len(self.streams[eng])
        self.streams[eng].append(o)
        self._mark((("e", eng), idx), reads, writes)
        return o

    def dma(self, queue, out, in_, reads=(), writes=(), key=None, **kw):
        o = _Op()
        o.signal = False
        o.dkey = key
        o.waits = self._deps(queue, reads, writes)
        if key not in self.dcount:
            self.dcount[key] = 0
            self.dorder.append(key)
        self.dcount[key] += 16
        o.fn = lambda e: e.dma_start(out=out, in_=in_, **kw)
        self.streams[queue].append(o)
        self._mark((("d", key), self.dcount[key]), reads, writes)
        return o

    def emit(self):
        nc = self.nc
        fin = []
        for e in ENGS:
            if e != "sp":
                idxs = [i for i, o in enumerate(self.streams[e]) if o.dkey is None]
                if idxs:
                    self.streams[e][idxs[-1]].signal = True
                    fin.append((("e", e), idxs[-1]))
        for k in self.dorder:
            fin.append((("d", k), self.dcount[k]))
        o = _Op()
        o.fn = None
        o.signal = False
        o.dkey = None
        o.waits = [(k, v) for k, v in fin if self.waited["sp"].get(k, -1) < v]
        self.streams["sp"].append(o)

        esem = {}
        cum = {}
        for e in ENGS:
            if e == "sp":
                continue
            esem[e] = nc.alloc_semaphore(name=f"s_{e}")
            c = 0
            arr = []
            for o in self.streams[e]:
                if o.signal:
                    c += 1
                arr.append(c)
            cum[e] = arr
        dsem = {}
        for i, k in enumerate(self.dorder):
            dsem[k] = nc.alloc_semaphore(name=f"d_{i}")
        self.n_sems = len(esem) + len(dsem)

        def run(ename, eng):
            for o in self.streams[ename]:
                for k, v in o.waits:
                    if k[0] == "e":
                        eng.wait_ge(esem[k[1]], cum[k[1]][v])
                    else:
                        eng.wait_ge(dsem[k[1]], v)
                if o.fn is None:
                    continue
                inst = o.fn(eng)
                if o.dkey is not None:
                    inst.then_inc(dsem[o.dkey], 16)
                elif o.signal:
                    inst.then_inc(esem[ename], 1)

        with nc.Block() as block:

            @block.tensor
            def _(eng):
                run("pe", eng)

            @block.scalar
            def _(eng):
                run("act", eng)

            @block.vector
            def _(eng):
                run("dve", eng)

            @block.gpsimd
            def _(eng):
                run("pool", eng)

            @block.sync
            def _(eng):
                run("sp", eng)

    def stats(self):
        return {e: len(s) for e, s in self.streams.items()}


class Buf:
    def __init__(self, t, toks, off=0):
        self.t = t
        self.toks = toks
        self.off = off

    @property
    def tok(self):
        return self.toks[0]


class Arena:
    def __init__(self, nc, base, limit):
        self.nc = nc
        self.cur = base
        self.limit = limit
        self.records = []
        self.n = 0
        self.peak = base

    def alloc(self, name, shape, dtype, ntok=1):
        nbytes = int(np.prod(shape[1:])) * mybir.dt.size(dtype)
        nbytes = (nbytes + 31) // 32 * 32
        off = self.cur
        assert off + nbytes <= self.limit, f"SBUF overflow allocating {name}: {off}+{nbytes} > {self.limit}"
        self.n += 1
        t = self.nc.alloc_sbuf_tensor_at(f"{name}_{self.n}", list(shape), dtype, offset=off)
        self.cur = off + nbytes
        self.peak = max(self.peak, self.cur)
        toks = [Tok(f"{name}{i}") for i in range(ntok)]
        inherit = {}
        for (s, e, otoks) in self.records:
            if s < off + nbytes and off < e:
                for ot in otoks:
                    if ot.lastw is not None:
                        k, v = ot.lastw
                        if inherit.get(k, -1) < v:
                            inherit[k] = v
                    for k, v in ot.readers.items():
                        if inherit.get(k, -1) < v:
                            inherit[k] = v
        for tk in toks:
            tk.readers = dict(inherit)
        self.records.append((off, off + nbytes, toks))
        return Buf(t, toks, off)

    def mark(self):
        return self.cur

    def release(self, m):
        self.cur = m


class Ring:
    def __init__(self, bufs):
        self.bufs = bufs
        self.i = 0

    def next(self):
        b = self.bufs[self.i % len(self.bufs)]
        self.i += 1
        return b


def _vec_layout():
    cols = {}
    c = 0

    def add(name, n):
        nonlocal c
        cols[name] = c
        c += n

    add("a_norm_g", 16)
    add("a_q_norm_g", 6)
    add("a_kv_norm_g", 2)
    add("a_pool_scale", 16)
    add("r_norm_g", 16)
    add("r_gn_g", 32)
    add("final_g", 8)
    add("decay", 16)
    add("inv64", 1)
    add("inv256", 1)
    add("poolcnt", 64)
    return cols, c


VCOL, NV = _vec_layout()
POOL_W = (2, 4, 8, 16)


def _build_vecs(inp, T):
    v = np.zeros((128, NV), np.float32)

    def put(name, arr, nchunk):
        a = np.asarray(arr, np.float32)
        L = a.shape[0]
        a = a.reshape(L, nchunk, 128).transpose(2, 0, 1).reshape(128, L * nchunk)
        v[:, VCOL[name]:VCOL[name] + L * nchunk] = a

    put("a_norm_g", inp["a_norm_g"], 8)
    put("a_q_norm_g", inp["a_q_norm_g"], 3)
    put("a_kv_norm_g", inp["a_kv_norm_g"], 1)
    put("a_pool_scale", inp["a_pool_scale"], 8)
    put("r_norm_g", inp["r_norm_g"], 8)
    put("r_gn_g", inp["r_gn_g"], 16)
    put("final_g", np.asarray(inp["final_norm_g"])[None, :], 8)
    dec = np.stack([np.asarray(inp["r_decay_fwd"], np.float32), np.asarray(inp["r_decay_bwd"], np.float32)], axis=1)
    v[:, VCOL["decay"]:VCOL["decay"] + 16] = np.broadcast_to(dec.reshape(1, 16), (128, 16))
    inv64 = (1.0 / (10000.0 ** (np.arange(0, 64, 2, dtype=np.float32) / np.float32(64)))).astype(np.float32)
    inv256 = (1.0 / (10000.0 ** (np.arange(0, 256, 2, dtype=np.float32) / np.float32(256)))).astype(np.float32)
    v[:, VCOL["inv64"]] = inv64[np.arange(128) % 32]
    v[:, VCOL["inv256"]] = inv256
    for g, w in enumerate(POOL_W):
        left = w // 2
        right = w - 1 - left
        t = np.arange(T)
        lo = np.clip(t - left, 0, T - 1)
        hi = np.clip(t + right, 0, T - 1)
        cnt = (hi - lo + 1).astype(np.float32)
        e = np.ones(16, np.float32)
        e[:8] = 1.0 / cnt[:8]
        e[8:] = 1.0 / cnt[T - 8:]
        v[:, VCOL["poolcnt"] + g * 16: VCOL["poolcnt"] + (g + 1) * 16] = e[None, :]
    return v


class Prog:
    def __init__(self, T, nseq, layers, final_norm, debug=False):
        self.T = T
        self.nseq = nseq
        self.layers = layers
        self.final_norm = final_norm
        self.NTG = T // 512
        self.NKT = T // 128
        self.GW = 2 * T - 128
        self.debug = debug
        nc = bass.Bass("TRN2", target_bir_lowering=False)
        self.nc = nc
        self.S = Sched(nc)
        self.pref = {}
        self._declare_dram()
        self.A = Arena(nc, 24576, 229376)
        self._alloc_static()
        self.prologue()
        for s in range(nseq):
            self.sequence(s)
        self.S.emit()

    def _declare_dram(self):
        nc, T, nseq = self.nc, self.T, self.nseq
        ein = lambda name, shape, dt=F32: nc.dram_tensor(name, list(shape), dt, kind="ExternalInput").ap()
        self.x_in = ein("x", [nseq, T, D])
        self.pos_in = ein("pos", [nseq, T], I32)
        self.vecs_in = ein("vecs", [128, NV])
        self.ident_in = ein("ident", [128, 128])
        self.dtab_in = ein("dtab", [128, self.GW])
        self.a_w_in = ein("a_w_in", [2, D, 3648])
        self.a_w_uq = ein("a_w_uq", [2, 384, 1536])
        self.a_w_ukv = ein("a_w_ukv", [2, 128, 2048])
        self.a_pool_w = ein("a_pool_w", [2, 4, 256, 256])
        self.a_w_out = ein("a_w_out", [2, 2048, D])
        self.r_w_in = ein("r_w_in", [2, D, 6144])
        self.r_w_out = ein("r_w_out", [2, 2048, D])
        self.out = nc.dram_tensor("out", [nseq, T, D], F32, kind="ExternalOutput").ap()
        self.xT = [nc.dram_tensor(f"xT{s}", [KC, 128, T], F32).ap() for s in range(nseq)]
        self.xT_tok = [[[Tok(f"xT{s}_{c}_{g}") for g in range(self.NTG)] for c in range(KC)] for s in range(nseq)]
        self.wb = {}

        def blk(name, kc, ncols):
            ap = nc.dram_tensor("wb_" + name, [128, kc * ncols], BF16).ap()
            self.wb[name] = (ap, Tok("wb_" + name), kc, ncols)

        for l in range(2):
            for g in range(4):
                blk(f"e1_{l}_{g}", 8, 512)
            blk(f"pw_{l}", 8, 256)
            for hf in range(2):
                blk(f"wob{hf}_{l}", 8, 512)
                blk(f"ga{hf}_{l}", 8, 512)
                blk(f"woa{hf}_{l}", 8, 512)
            blk(f"e2_{l}", 8, 640)
            blk(f"uq_{l}", 3, 2048)
            blk(f"ukv_{l}", 1, 2048)
            for h in range(4):
                blk(f"rk_{l}_{h}", 8, 768)
                blk(f"rq_{l}_{h}", 8, 768)
                blk(f"rwo_{l}_{h}", 4, 1024)
        self.gtab = nc.dram_tensor("gtab", [2, 4, 128, self.GW], BF16).ap()
        self.gtab_tok = [[Tok(f"g{l}{h}") for h in range(4)] for l in range(2)]
        self.x_in_tok = Tok("x_in")
        self.out_tok = Tok("out")
        self.const_tok = Tok("const_in")

    def _alloc_static(self):
        A, nc, S = self.A, self.nc, self.S
        self.vecs = A.alloc("vecs", [128, NV], F32)
        self.ident = A.alloc("ident", [128, 128], F32)
        self.ones = A.alloc("ones", [128, 128], BF16)
        self.ones_d = A.alloc("ones_d", [128, 128], BF16)
        self.ones_v = A.alloc("ones_v", [128, 128], BF16)
        self.ldec = A.alloc("ldec", [128, 16], F32)
        self.ones_f = A.alloc("ones_f", [128, 128], F32)
        S.dma("sp", self.vecs.t[:], self.vecs_in, writes=[self.vecs.tok], key="const")
        S.dma("sp", self.ident.t[:], self.ident_in, writes=[self.ident.tok], key="const")
        self.vecs.tok.lastw = self.ident.tok.lastw = (("d", "const"), S.dcount["const"])
        S.op("pool", lambda e: e.memset(self.ones.t[:], 1.0), writes=[self.ones.tok])
        S.op("pool", lambda e: e.memset(self.ones_d.t[:], 1.0 / 1024.0), writes=[self.ones_d.tok])
        S.op("pool", lambda e: e.memset(self.ones_v.t[:], 1.0 / 512.0), writes=[self.ones_v.tok])
        S.op("pool", lambda e: e.memset(self.ones_f.t[:], 1.0), writes=[self.ones_f.tok])
        self.ps = []
        for i in range(8):
            t = nc.alloc_psum_tensor(f"ps{i}", [128, 512], F32)
            self.ps.append(Buf(t, [Tok(f"ps{i}")]))
        self.ps_gen = Ring(self.ps[0:4])
        self.ps_acc = self.ps[4:8]
        self.wring = Ring([A.alloc(f"wr{i}", [128, 6144], BF16) for i in range(4)])
        self.static_mark = A.mark()

    def dump(self, name, ap, toks):
        if not self.debug:
            return
        shape = list(ap.shape)
        d = self.nc.dram_tensor("dbg_" + name, shape, ap.dtype, kind="ExternalOutput").ap()
        self.S.dma("sp", d, ap, reads=list(toks), key=("dbg", name))

    def vcol(self, name, i=0):
        c = VCOL[name] + i
        return self.vecs.t[:, c:c + 1]

    def mm(self, out, lhsT, rhs, start, stop, reads, writes):
        self.S.op("pe", lambda e: e.matmul(out, lhsT, rhs, start=start, stop=stop), reads=reads, writes=writes)

    def prefetch(self, *names):
        for name in names:
            if name is not None and name in self.wb and name not in self.pref:
                self.pref[name] = self._load_w(name)

    def load_w(self, name):
        if name in self.pref:
            return self.pref.pop(name)
        return self._load_w(name)

    def next_blocks(self, nxt):
        if nxt is None:
            return (None, None, None)
        i = nxt // 2
        if nxt % 2 == 0:
            return (f"e1_{i}_0", f"pw_{i}", None)
        return (f"rk_{i}_0", f"rq_{i}_0", f"rwo_{i}_0")

    def _load_w(self, name):
        ap, tok, kc, ncols = self.wb[name]
        slot = self.wring.next()
        n = kc * ncols
        nsplit = max(1, n // 4096)
        step = n // nsplit
        for i in range(nsplit):
            self.S.dma("sp", slot.t[:, i * step:(i + 1) * step], ap[:, i * step:(i + 1) * step], reads=[tok], writes=[slot.tok], key=("wr", slot.off))
        view = slot.t[:, 0:kc * ncols].rearrange("p (k n) -> p k n", k=kc)
        return view, slot.tok

    def rstd_from_ps(self, ps, n_scale, tmp, out):
        S = self.S
        S.op("act", lambda e: e.activation(tmp.t[:], ps.t[:], AF.Ln, bias=EPS, scale=n_scale),
             reads=[ps.tok], writes=[tmp.tok])
        S.op("act", lambda e: e.activation(out.t[:], tmp.t[:], AF.Exp, scale=-0.5),
             reads=[tmp.tok], writes=[out.tok])

    def prologue(self):
        S, A, nc, T = self.S, self.A, self.nc, self.T
        m0 = A.mark()
        need_even = any(l % 2 == 0 for l in self.layers)
        need_odd = any(l % 2 == 1 for l in self.layers)
        ev = sorted(set(l // 2 for l in self.layers if l % 2 == 0))
        od = sorted(set(l // 2 for l in self.layers if l % 2 == 1))

        def cast(name, src3, key):
            ap, tok, kc, ncols = self.wb[name]
            dst = ap.rearrange("p (k n) -> p k n", k=kc)
            S.dma("pool", dst, src3, reads=[self.const_tok], writes=[tok], key=key)

        def rows(w2d, c0, n):
            return w2d[:, c0:c0 + n].rearrange("(k p) n -> p k n", p=128)

        def cast_even(i):
            key = ("cast", "e1", i)
            toks = []
            win = self.a_w_in[i]
            for g in range(4):
                ap, tok, kc, ncols = self.wb[f"e1_{i}_{g}"]
                dst = ap.rearrange("p (k n) -> p k n", k=8)
                S.dma("pool", dst[:, :, 0:256], rows(win, 576 + g * 256, 256), writes=[tok], key=key)
                S.dma("pool", dst[:, :, 256:512], rows(win, 576 + 1024 + 1024 + g * 256, 256), writes=[tok], key=key)
                toks.append(tok)
            ap, tok, kc, ncols = self.wb[f"pw_{i}"]
            S.dma("pool", ap.rearrange("p (k n) -> p k n", k=8),
                  self.a_pool_w[i].rearrange("g (k p) n -> p (g k) n", p=128), writes=[tok], key=key)
            toks.append(tok)
            for tk in toks:
                tk.lastw = (("d", key), S.dcount[key])
            key = ("cast", "e", i)
            toks = []
            for hf in range(2):
                cast(f"wob{hf}_{i}", self.a_w_out[i, 1024:2048, hf * 512:(hf + 1) * 512].rearrange("(k p) n -> p k n", p=128), key)
                cast(f"woa{hf}_{i}", self.a_w_out[i, 0:1024, hf * 512:(hf + 1) * 512].rearrange("(k p) n -> p k n", p=128), key)
                cast(f"ga{hf}_{i}", rows(win, 576 + 1024 + hf * 512, 512), key)
            cast(f"ukv_{i}", self.a_w_ukv[i].rearrange("(k p) n -> p k n", p=128), key)
            toks += [self.wb[f"{n}_{i}"][1] for n in ("wob0", "wob1", "woa0", "woa1", "ga0", "ga1", "ukv")]
            mm_ = A.mark()
            stg = A.alloc("stg_e2", [128, 8, 576], F32)
            ob = A.alloc("ob_e2", [128, 8, 640], BF16)
            S.dma("sp", stg.t[:], rows(win, 0, 576), writes=[stg.tok], key="pro_ld_e2")
            S.op("dve", lambda e, stg=stg, ob=ob: e.tensor_copy(ob.t[:, :, 0:576], stg.t[:]), reads=[stg.tok], writes=[ob.tok])
            S.op("act", lambda e, stg=stg, ob=ob: e.mul(ob.t[:, :, 576:608], stg.t[:, :, 544:576], -1.0), reads=[stg.tok], writes=[ob.tok])
            S.op("act", lambda e, stg=stg, ob=ob: e.copy(ob.t[:, :, 608:640], stg.t[:, :, 512:544]), reads=[stg.tok], writes=[ob.tok])
            ap, tok, kc, ncols = self.wb[f"e2_{i}"]
            S.dma("sp", ap.rearrange("p (k n) -> p k n", k=8), ob.t[:], reads=[ob.tok], writes=[tok], key="pro_st_e2")
            A.release(mm_)
            mm_ = A.mark()
            stg = A.alloc("stg_uq", [128, 3, 1536], F32)
            ob = A.alloc("ob_uq", [128, 3, 2048], BF16)
            S.dma("sp", stg.t[:], self.a_w_uq[i].rearrange("(k p) n -> p k n", p=128), writes=[stg.tok], key="pro_ld_uq")
            sv = stg.t[:].rearrange("p k (h c) -> p k h c", h=8)
            ov = ob.t[:].rearrange("p k (h c) -> p k h c", h=8)
            for k3 in range(3):
                S.op("dve", lambda e, k3=k3, sv=sv, ov=ov: e.tensor_copy(ov[:, k3, :, 0:192], sv[:, k3, :, :]), reads=[stg.tok], writes=[ob.tok])
                S.op("act", lambda e, k3=k3, sv=sv, ov=ov: e.mul(ov[:, k3, :, 192:224], sv[:, k3, :, 160:192], -1.0), reads=[stg.tok], writes=[ob.tok])
                S.op("act", lambda e, k3=k3, sv=sv, ov=ov: e.copy(ov[:, k3, :, 224:256], sv[:, k3, :, 128:160]), reads=[stg.tok], writes=[ob.tok])
            ap, tok, kc, ncols = self.wb[f"uq_{i}"]
            S.dma("sp", ap.rearrange("p (k n) -> p k n", k=3), ob.t[:], reads=[ob.tok], writes=[tok], key="pro_st_uq")
            A.release(mm_)
            for tk in toks:
                tk.lastw = (("d", key), S.dcount[key])
        def cast_odd(i):
            key = ("cast", "o", i)
            toks = []
            win = self.r_w_in[i]
            for h in range(4):
                ap, tok, kc, ncols = self.wb[f"rk_{i}_{h}"]
                dst = ap.rearrange("p (k n) -> p k n", k=8)
                S.dma("pool", dst[:, :, 0:256], rows(win, 1024 + h * 256, 256), writes=[tok], key=key)
                S.dma("pool", dst[:, :, 256:768], rows(win, 2048 + h * 512, 512), writes=[tok], key=key)
                toks.append(tok)
                ap, tok, kc, ncols = self.wb[f"rq_{i}_{h}"]
                dst = ap.rearrange("p (k n) -> p k n", k=8)
                S.dma("pool", dst[:, :, 0:256], rows(win, h * 256, 256), writes=[tok], key=key)
                S.dma("pool", dst[:, :, 256:768], rows(win, 4096 + h * 512, 512), writes=[tok], key=key)
                toks.append(tok)
                cast(f"rwo_{i}_{h}", self.r_w_out[i, h * 512:(h + 1) * 512, :].rearrange("(k p) n -> p k n", p=128), key)
                toks.append(self.wb[f"rwo_{i}_{h}"][1])
            for tk in toks:
                tk.lastw = (("d", key), S.dcount[key])
        for i in sorted(set(ev) | set(od)):
            if i in ev:
                cast_even(i)
            if i in od:
                cast_odd(i)
        if od:
            dc = VCOL["decay"]
            S.op("act", lambda e: e.activation(self.ldec.t[:], self.vecs.t[:, dc:dc + 16], AF.Sigmoid),
                 reads=[self.vecs.tok], writes=[self.ldec.tok])
            S.op("act", lambda e: e.activation(self.ldec.t[:], self.ldec.t[:], AF.Ln),
                 reads=[self.ldec.tok], writes=[self.ldec.tok])
            mm_ = A.mark()
            GW = self.GW
            dt_ = A.alloc("dtab", [128, GW], F32)
            rp = A.alloc("rp", [128, GW], F32)
            rn = A.alloc("rn", [128, GW], F32)
            S.dma("sp", dt_.t[:], self.dtab_in, writes=[dt_.tok], key="pro_ld_dt")
            S.op("dve", lambda e: e.tensor_scalar(rp.t[:], dt_.t[:], 0.0, None, ALU.max), reads=[dt_.tok], writes=[rp.tok])
            S.op("dve", lambda e: e.tensor_scalar(rn.t[:], dt_.t[:], -1.0, 0.0, ALU.mult, ALU.max), reads=[dt_.tok], writes=[rn.tok])
            e1 = A.alloc("e1", [128, GW], F32)
            e2 = A.alloc("e2", [128, GW], F32)
            gb = A.alloc("gb", [128, GW], BF16)
            for i in od:
                for h in range(4):
                    cf = i * 8 + h
                    cb = i * 8 + 4 + h
                    S.op("act", lambda e, cf=cf: e.activation(e1.t[:], rp.t[:], AF.Exp, scale=self.ldec.t[:, cf:cf + 1]),
                         reads=[rp.tok, self.ldec.tok], writes=[e1.tok])
                    S.op("act", lambda e, cb=cb: e.activation(e2.t[:], rn.t[:], AF.Exp, scale=self.ldec.t[:, cb:cb + 1]),
                         reads=[rn.tok, self.ldec.tok], writes=[e2.tok])
                    S.op("dve", lambda e: e.scalar_tensor_tensor(gb.t[:], e1.t[:], 0.0625, e2.t[:], ALU.mult, ALU.mult),
                         reads=[e1.tok, e2.tok], writes=[gb.tok])
                    S.dma("sp", self.gtab[i, h], gb.t[:], reads=[gb.tok], writes=[self.gtab_tok[i][h]], key="pro_st_g")
            A.release(mm_)
        A.release(m0)

    def sequence(self, s):
        A = self.A
        m0 = A.mark()
        if s == 0:
            for _ in self.load_x(s):
                pass
            A.release(m0)
        for n, l in enumerate(self.layers):
            if n + 1 < len(self.layers):
                nxt = self.layers[n + 1]
            elif s + 1 < self.nseq:
                nxt = self.layers[0]
            else:
                nxt = None
            if l % 2 == 0:
                self.even_layer(s, l // 2, nxt)
            else:
                self.odd_layer(s, l // 2, nxt)
        g1 = self.store_out(s)
        g2 = self.load_x(s + 1) if s + 1 < self.nseq else iter(())
        done1 = done2 = False
        while not (done1 and done2):
            if not done1:
                done1 = next(g1, "end") == "end"
            if not done2:
                done2 = next(g2, "end") == "end"
        A.release(m0)

    def load_x(self, s):
        S, A, T = self.S, self.A, self.T
        m0 = A.mark()
        stg = Ring([A.alloc(f"xs{i}", [128, 4, D], F32) for i in range(2)])
        ost = Ring([A.alloc(f"xo{i}", [128, 512], F32) for i in range(4)])
        for tg in range(self.NTG):
            sb = stg.next()
            src = self.x_in[s, tg * 512:(tg + 1) * 512, :].rearrange("(j p) d -> p j d", p=128)
            S.dma("sp", sb.t[:], src, reads=[self.x_in_tok], writes=[sb.tok], key=("xs", sb.off))
            for c in range(KC):
                ps = self.ps_gen.next()
                for j in range(4):
                    S.op("pe", lambda e, ps=ps, sb=sb, j=j, c=c: e.transpose(ps.t[:, j * 128:(j + 1) * 128], sb.t[:, j, c * 128:(c + 1) * 128], self.ident.t[:]),
                         reads=[sb.tok, self.ident.tok], writes=[ps.tok])
                ob = ost.next()
                eng = "act" if c % 2 == 0 else "dve"
                if eng == "act":
                    S.op("act", lambda e, ob=ob, ps=ps: e.copy(ob.t[:], ps.t[:]), reads=[ps.tok], writes=[ob.tok])
                else:
                    S.op("dve", lambda e, ob=ob, ps=ps: e.tensor_copy(ob.t[:], ps.t[:]), reads=[ps.tok], writes=[ob.tok])
                S.dma("sp", self.xT[s][c, :, tg * 512:(tg + 1) * 512], ob.t[:], reads=[ob.tok],
                      writes=[self.xT_tok[s][c][tg]], key=("xo", ob.off))
            yield tg

    def rope_tables(self, s, kind):
        S, A, T = self.S, self.A, self.T
        cs = A.alloc("cs", [128, T], F32)
        sn = A.alloc("sn", [128, T], F32)
        if kind == 64:
            self.cs64, self.sn64 = cs, sn
        else:
            self.cs256, self.sn256 = cs, sn
        m0 = A.mark()
        pi_ = A.alloc("pos_i", [128, T], I32)
        pf = A.alloc("pos_f", [128, T], F32)
        ang = A.alloc("ang", [128, T], F32)
        kf = A.alloc("kf", [128, T], F32)
        ki = A.alloc("ki", [128, T], I32)
        S.dma("sp", pi_.t[:], self.pos_in[s:s + 1, :].partition_broadcast(128), reads=[self.x_in_tok], writes=[pi_.tok], key="pos")
        S.op("dve", lambda e: e.tensor_copy(pf.t[:], pi_.t[:]), reads=[pi_.tok], writes=[pf.tok])
        TWO_PI = 2.0 * np.pi
        C1 = 6.28125
        C2 = TWO_PI - C1
        for (name, cs, sn) in ((f"inv{kind}", cs, sn),):
            inv = self.vcol(name)
            S.op("dve", lambda e, inv=inv: e.tensor_scalar(ang.t[:], pf.t[:], inv, None, ALU.mult), reads=[pf.tok, self.vecs.tok], writes=[ang.tok])
            S.op("dve", lambda e: e.tensor_scalar(kf.t[:], ang.t[:], 1.0 / TWO_PI, None, ALU.mult), reads=[ang.tok], writes=[kf.tok])
            S.op("dve", lambda e: e.tensor_copy(ki.t[:], kf.t[:]), reads=[kf.tok], writes=[ki.tok])
            S.op("dve", lambda e: e.tensor_copy(kf.t[:], ki.t[:]), reads=[ki.tok], writes=[kf.tok])
            S.op("dve", lambda e: e.scalar_tensor_tensor(ang.t[:], kf.t[:], -C1, ang.t[:], ALU.mult, ALU.add), reads=[kf.tok, ang.tok], writes=[ang.tok])
            S.op("dve", lambda e: e.scalar_tensor_tensor(ang.t[:], kf.t[:], -C2, ang.t[:], ALU.mult, ALU.add), reads=[kf.tok, ang.tok], writes=[ang.tok])
            S.op("dve", lambda e: e.tensor_scalar(ang.t[:], ang.t[:], np.pi, -np.pi, ALU.min, ALU.max), reads=[ang.tok], writes=[ang.tok])
            S.op("act", lambda e, sn=sn: e.activation(sn.t[:], ang.t[:], AF.Sin), reads=[ang.tok], writes=[sn.tok])
            S.op("dve", lambda e: e.tensor_scalar(kf.t[:], ang.t[:], np.pi / 2, np.pi, ALU.add, ALU.is_gt), reads=[ang.tok], writes=[kf.tok])
            S.op("dve", lambda e: e.scalar_tensor_tensor(ang.t[:], kf.t[:], -TWO_PI, ang.t[:], ALU.mult, ALU.add), reads=[kf.tok, ang.tok], writes=[ang.tok])
            S.op("dve", lambda e: e.tensor_scalar(ang.t[:], ang.t[:], np.pi / 2, np.pi, ALU.add, ALU.min), reads=[ang.tok], writes=[ang.tok])
            S.op("act", lambda e, cs=cs: e.activation(cs.t[:], ang.t[:], AF.Sin), reads=[ang.tok], writes=[cs.tok])
        A.release(m0)

    def make_h(self, s, gname, li):
        S, A, T = self.S, self.A, self.T
        hT = A.alloc("hT", [128, KC, T], BF16, ntok=KC * self.NTG)
        self.hT = hT
        htok = lambda c, g: hT.toks[c * self.NTG + g]
        self.htok = htok
        m0 = A.mark()
        xl = Ring([A.alloc(f"xl{i}", [128, KC, 512], F32) for i in range(2)])
        sq = Ring([A.alloc(f"sq{i}", [128, 512], BF16) for i in range(3)])
        tmp = Ring([A.alloc(f"lt{i}", [128, 512], F32) for i in range(1)])
        rs = Ring([A.alloc(f"rs{i}", [128, 512], F32) for i in range(1)])
        for tg in range(self.NTG):
            xb = xl.next()
            src = self.xT[s][:, :, tg * 512:(tg + 1) * 512].rearrange("c p t -> p c t")
            S.dma("sp", xb.t[:], src, reads=[self.xT_tok[s][c][tg] for c in range(KC)], writes=[xb.tok], key=("xl", xb.off))
            ps = self.ps_gen.next()
            for c in range(KC):
                q = sq.next()
                S.op("act", lambda e, q=q, xb=xb, c=c: e.activation(q.t[:], xb.t[:, c, :], AF.Square), reads=[xb.tok], writes=[q.tok])
                self.mm(ps.t[:], self.ones_d.t[:], q.t[:], c == 0, c == KC - 1, [self.ones_d.tok, q.tok], [ps.tok])
            t_ = tmp.next()
            r_ = rs.next()
            self.rstd_from_ps(ps, 1.0, t_, r_)
            for c in range(KC):
                eng = "dve"
                g = self.vcol(gname, li * 8 + c)
                S.op(eng, lambda e, xb=xb, c=c, g=g, r_=r_, tg=tg: e.scalar_tensor_tensor(
                    hT.t[:, c, tg * 512:(tg + 1) * 512], xb.t[:, c, :], g, r_.t[:], ALU.mult, ALU.mult),
                    reads=[xb.tok, r_.tok, self.vecs.tok], writes=[htok(c, tg)])
        A.release(m0)

    def acc_begin(self, s, tiles, ring, la=2):
        st = {"s": s, "tiles": tiles, "ring": ring, "la": la, "pend": [], "i": 0, "n": 0}
        for _ in range(min(la, len(tiles))):
            self._acc_issue(st)
        return st

    def _acc_issue(self, st):
        oc, tg = st["tiles"][st["n"]]
        st["n"] += 1
        s = st["s"]
        ob = st["ring"].next()
        tk = self.xT_tok[s][oc][tg]
        dst = self.xT[s][oc, :, tg * 512:(tg + 1) * 512]
        self.S.dma("sp", ob.t[:], dst, reads=[tk], writes=[ob.tok], key=("acc", ob.off))
        st["pend"].append((ob, tk, dst))

    def acc_step(self, st, ps):
        S = self.S
        if st["n"] < len(st["tiles"]):
            self._acc_issue(st)
        ob, tk, dst = st["pend"].pop(0)
        S.op("dve", lambda e: e.tensor_tensor(ob.t[:], ps.t[:], ob.t[:], ALU.add), reads=[ps.tok, ob.tok], writes=[ob.tok])
        S.dma("sp", dst, ob.t[:], reads=[ob.tok], writes=[tk], key=("acc", ob.off))

    def accum_x(self, s, oc, tg, ps, stg_ring, ev_ring=None):
        S = self.S
        ob = stg_ring.next()
        tk = self.xT_tok[s][oc][tg]
        dst = self.xT[s][oc, :, tg * 512:(tg + 1) * 512]
        S.dma("sp", ob.t[:], dst, reads=[tk], writes=[ob.tok], key=("acc", ob.off))
        if ev_ring is None:
            S.op("dve", lambda e: e.tensor_tensor(ob.t[:], ps.t[:], ob.t[:], ALU.add), reads=[ps.tok, ob.tok], writes=[ob.tok])
        else:
            ev = ev_ring.next()
            S.op("act", lambda e: e.copy(ev.t[:], ps.t[:]), reads=[ps.tok], writes=[ev.tok])
            S.op("pool", lambda e: e.tensor_tensor(ob.t[:], ev.t[:], ob.t[:], ALU.add), reads=[ev.tok, ob.tok], writes=[ob.tok])
        S.dma("sp", dst, ob.t[:], reads=[ob.tok], writes=[tk], key=("acc", ob.off))

    def even_layer(self, s, li, nxt=None):
        S, A, T, NTG, NKT = self.S, self.A, self.T, self.NTG, self.NKT
        m_layer = A.mark()
        self.rope_tables(s, 64)
        ym = A.alloc("ym", [128, 8, T], BF16, ntok=8 * NTG)
        ymtok = lambda c, g: ym.toks[c * NTG + g]
        accst = Ring([A.alloc(f"acs{i}", [128, 512], F32) for i in range(4)])
        cqn = A.alloc("cqn", [128, 3, T], BF16, ntok=3 * NTG)
        ckvn = A.alloc("ckvn", [128, T], BF16, ntok=NTG)
        krope = A.alloc("krope", [128, T], BF16, ntok=NTG)
        m_h = A.mark()
        self.make_h(s, "a_norm_g", li)
        hT, htok = self.hT, self.htok

        m1 = A.mark()
        PADL = 8
        TP = T + 16
        U = Ring([A.alloc(f"U{i}", [128, TP], F32) for i in range(1)])
        TA = A.alloc("TA", [128, TP], F32)
        TB = A.alloc("TB", [128, TP], F32)
        dT = A.alloc("dT", [128, 2, T], BF16, ntok=2)
        sg = Ring([A.alloc(f"sgb{i}", [128, 512], BF16) for i in range(2)])
        for g in range(4):
            w = POOL_W[g]
            right = w - 1 - w // 2
            wv, wtok = self.load_w(f"e1_{li}_{g}")
            pw_v, pw_tok = self.load_w(f"pw_{li}")
            for cc in range(2):
                u = U.next()
                S.op("pool", lambda e, u=u: e.memset(u.t[:, 0:PADL], 0.0), writes=[u.tok])
                S.op("pool", lambda e, u=u: e.memset(u.t[:, PADL + T:TP], 0.0), writes=[u.tok])
                for tg in range(NTG):
                    ps = self.ps_gen.next()
                    for kc in range(KC):
                        self.mm(ps.t[:], wv[:, kc, cc * 128:(cc + 1) * 128], hT.t[:, kc, tg * 512:(tg + 1) * 512],
                                kc == 0, kc == KC - 1, [wtok, htok(kc, tg)], [ps.tok])
                    S.op("act", lambda e, u=u, ps=ps, tg=tg: e.copy(u.t[:, PADL + tg * 512:PADL + (tg + 1) * 512], ps.t[:]),
                         reads=[ps.tok], writes=[u.tok])
                if s == 0:
                    self.dump(f"U_{g}_{cc}", u.t[:], [u.tok])
                cur = u
                span = 1
                bufs = [TA, TB]
                bi = 0
                lo = -PADL + 1
                while span * 2 < w:
                    dst = bufs[bi]
                    bi ^= 1
                    a0 = lo + PADL
                    n = TP - a0
                    S.op("dve", lambda e, dst=dst, cur=cur, a0=a0, n=n, span=span: e.tensor_tensor(
                        dst.t[:, a0:a0 + n], cur.t[:, a0:a0 + n], cur.t[:, a0 - span:a0 - span + n], ALU.add),
                        reads=[cur.tok], writes=[dst.tok])
                    cur = dst
                    span *= 2
                    lo += span
                dst = bufs[bi]
                S.op("dve", lambda e, dst=dst, cur=cur, right=right: e.tensor_tensor(
                    dst.t[:, PADL:PADL + T], cur.t[:, PADL - 1:PADL - 1 + T], cur.t[:, PADL + right:PADL + right + T], ALU.add),
                    reads=[cur.tok], writes=[dst.tok])
                S.op("dve", lambda e, dst=dst, u=u, cc=cc, w=w: e.scalar_tensor_tensor(
                    dT.t[:, cc, :], dst.t[:, PADL:PADL + T], 1.0 / w, u.t[:, PADL:PADL + T], ALU.mult, ALU.subtract),
                    reads=[dst.tok, u.tok], writes=[dT.toks[cc]])
                pc = VCOL["poolcnt"] + g * 16
                for (o0, c0) in ((0, pc), (T - 8, pc + 8)):
                    S.op("pool", lambda e, dst=dst, o0=o0, c0=c0: e.tensor_tensor(
                        dst.t[:, PADL + o0:PADL + o0 + 8], dst.t[:, PADL + o0:PADL + o0 + 8], self.vecs.t[:, c0:c0 + 8], ALU.mult),
                        reads=[dst.tok, self.vecs.tok], writes=[dst.tok])
                    S.op("pool", lambda e, dst=dst, u=u, o0=o0, cc=cc: e.tensor_tensor(
                        dT.t[:, cc, o0:o0 + 8], dst.t[:, PADL + o0:PADL + o0 + 8], u.t[:, PADL + o0:PADL + o0 + 8], ALU.subtract),
                        reads=[dst.tok, u.tok, dT.toks[cc]], writes=[dT.toks[cc]])
                ch = 2 * g + cc
                for tg in range(NTG):
                    psg = self.ps_gen.next()
                    for kc in range(KC):
                        self.mm(psg.t[:], wv[:, kc, 256 + cc * 128:256 + (cc + 1) * 128], hT.t[:, kc, tg * 512:(tg + 1) * 512],
                                kc == 0, kc == KC - 1, [wtok, htok(kc, tg)], [psg.tok])
                    S.op("act", lambda e, psg=psg, ch=ch, tg=tg: e.activation(ym.t[:, ch, tg * 512:(tg + 1) * 512], psg.t[:], AF.Silu),
                         reads=[psg.tok], writes=[ymtok(ch, tg)])
            if s == 0:
                self.dump(f"dT_{g}", dT.t[:], dT.toks)
            for j in range(2):
                ch = 2 * g + j
                for tg in range(NTG):
                    psy = self.ps_gen.next()
                    for cc in range(2):
                        self.mm(psy.t[:], pw_v[:, g * 2 + cc, j * 128:(j + 1) * 128], dT.t[:, cc, tg * 512:(tg + 1) * 512],
                                cc == 0, cc == 1, [pw_tok, dT.toks[cc]], [psy.tok])
                    sc = self.vcol("a_pool_scale", li * 8 + ch)
                    S.op("dve", lambda e, psy=psy, sc=sc, ch=ch, tg=tg: e.scalar_tensor_tensor(
                        ym.t[:, ch, tg * 512:(tg + 1) * 512], psy.t[:], sc, ym.t[:, ch, tg * 512:(tg + 1) * 512], ALU.mult, ALU.mult),
                        reads=[psy.tok, ymtok(ch, tg), self.vecs.tok], writes=[ymtok(ch, tg)])
        if s == 0:
            self.dump("ymb", ym.t[:], ym.toks)
            self.dump("hT", hT.t[:], hT.toks)
        A.release(m1)
        self.prefetch(f"wob0_{li}", f"wob1_{li}", f"e2_{li}", f"ga0_{li}")
        for hf in range(2):
            wv, wtok = self.load_w(f"wob{hf}_{li}")
            ast = self.acc_begin(s, [(hf * 4 + o4, tg) for tg in range(NTG) for o4 in range(4)], accst)
            for tg in range(NTG):
                for o4 in range(4):
                    ps = self.ps_gen.next()
                    for kc in range(8):
                        self.mm(ps.t[:], wv[:, kc, o4 * 128:(o4 + 1) * 128], ym.t[:, kc, tg * 512:(tg + 1) * 512],
                                kc == 0, kc == 7, [wtok, ymtok(kc, tg)], [ps.tok])
                    self.acc_step(ast, ps)
            self.prefetch(f"ga1_{li}" if hf == 0 else f"uq_{li}")
        m2 = A.mark()
        cqf = Ring([A.alloc(f"cqf{i}", [128, 4, 512], F32) for i in range(1)])
        sq = Ring([A.alloc(f"sq{i}", [128, 512], BF16) for i in range(4)])
        lt = Ring([A.alloc(f"lt{i}", [128, 512], F32) for i in range(2)])
        rsb = Ring([A.alloc(f"rs{i}", [128, 512], F32) for i in range(4)])
        rt = Ring([A.alloc(f"rt{i}", [128, 512], F32) for i in range(4)])
        wv, wtok = self.load_w(f"e2_{li}")
        for tg in range(NTG):
            tsl = slice(tg * 512, (tg + 1) * 512)
            cf = cqf.next()
            pss = self.ps_acc[0]
            psk = self.ps_acc[1]
            def sumsq(j, q):
                if j < 3:
                    self.mm(pss.t[:], self.ones.t[:], q.t[:], j == 0, j == 2, [self.ones.tok, q.tok], [pss.tok])
                else:
                    self.mm(psk.t[:], self.ones.t[:], q.t[:], True, True, [self.ones.tok, q.tok], [psk.tok])

            pend_sq = None
            for j in range(4):
                ps = self.ps_gen.next()
                for kc in range(KC):
                    self.mm(ps.t[:], wv[:, kc, j * 128:(j + 1) * 128], hT.t[:, kc, tsl], kc == 0, kc == KC - 1,
                            [wtok, htok(kc, tg)], [ps.tok])
                S.op("dve", lambda e, cf=cf, ps=ps, j=j: e.tensor_copy(cf.t[:, j, :], ps.t[:]), reads=[ps.tok], writes=[cf.tok])
                q = sq.next()
                S.op("act", lambda e, q=q, cf=cf, j=j: e.activation(q.t[:], cf.t[:, j, :], AF.Square), reads=[cf.tok], writes=[q.tok])
                if pend_sq is not None:
                    sumsq(*pend_sq)
                pend_sq = (j, q)
            psa = self.ps_gen.next()
            for kc in range(KC):
                self.mm(psa.t[0:64, :], wv[:, kc, 512:576], hT.t[:, kc, tsl], kc == 0, kc == KC - 1, [wtok, htok(kc, tg)], [psa.tok])
            sumsq(*pend_sq)
            psb = self.ps_gen.next()
            for kc in range(KC):
                self.mm(psb.t[0:64, :], wv[:, kc, 576:640], hT.t[:, kc, tsl], kc == 0, kc == KC - 1, [wtok, htok(kc, tg)], [psb.tok])
            t1, r1 = lt.next(), rsb.next()
            self.rstd_from_ps(pss, 1.0 / 384.0, t1, r1)
            t2, r2 = lt.next(), rsb.next()
            self.rstd_from_ps(psk, 1.0 / 128.0, t2, r2)
            for j in range(3):
                g = self.vcol("a_q_norm_g", li * 3 + j)
                S.op("dve", lambda e, cf=cf, j=j, g=g, r1=r1, tsl=tsl: e.scalar_tensor_tensor(
                    cqn.t[:, j, tsl], cf.t[:, j, :], g, r1.t[:], ALU.mult, ALU.mult),
                    reads=[cf.tok, r1.tok, self.vecs.tok], writes=[cqn.toks[j * NTG + tg]])
            g = self.vcol("a_kv_norm_g", li)
            S.op("dve", lambda e, cf=cf, g=g, r2=r2, tsl=tsl: e.scalar_tensor_tensor(
                ckvn.t[:, tsl], cf.t[:, 3, :], g, r2.t[:], ALU.mult, ALU.mult),
                reads=[cf.tok, r2.tok, self.vecs.tok], writes=[ckvn.toks[tg]])
            ta, tb = rt.next(), rt.next()
            S.op("dve", lambda e, ta=ta, psa=psa, tsl=tsl: e.tensor_tensor(ta.t[0:64, :], psa.t[0:64, :], self.cs64.t[0:64, tsl], ALU.mult),
                 reads=[psa.tok, self.cs64.tok], writes=[ta.tok])
            S.op("dve", lambda e, tb=tb, psb=psb, tsl=tsl: e.tensor_tensor(tb.t[0:64, :], psb.t[0:64, :], self.sn64.t[0:64, tsl], ALU.mult),
                 reads=[psb.tok, self.sn64.tok], writes=[tb.tok])
            S.op("pool", lambda e, ta=ta, tb=tb, tsl=tsl: e.tensor_tensor(krope.t[0:64, tsl], ta.t[0:64, :], tb.t[0:64, :], ALU.add),
                 reads=[ta.tok, tb.tok], writes=[krope.toks[tg]])
        self.prefetch(f"ukv_{li}")
        for hf in range(2):
            wv, wtok = self.load_w(f"ga{hf}_{li}")
            for tg in range(NTG):
                tsl = slice(tg * 512, (tg + 1) * 512)
                for c4 in range(4):
                    ch = hf * 4 + c4
                    ps = self.ps_gen.next()
                    for kc in range(KC):
                        self.mm(ps.t[:], wv[:, kc, c4 * 128:(c4 + 1) * 128], hT.t[:, kc, tsl], kc == 0, kc == KC - 1,
                                [wtok, htok(kc, tg)], [ps.tok])
                    S.op("act", lambda e, ps=ps, ch=ch, tsl=tsl: e.activation(ym.t[:, ch, tsl], ps.t[:], AF.Silu),
                         reads=[ps.tok], writes=[ymtok(ch, tg)])
            self.prefetch(f"woa{hf}_{li}")
        A.release(m2)

        A.release(m_h)
        uq_v, uq_tok = self.load_w(f"uq_{li}")
        ukv_v, ukv_tok = self.load_w(f"ukv_{li}")
        KT = Ring([A.alloc(f"KT{i}", [128, T], BF16, ntok=NTG) for i in range(2)])
        VH = Ring([A.alloc(f"VH{i}", [128, NKT, 128], BF16, ntok=NTG) for i in range(2)])
        QN = Ring([A.alloc(f"QN{i}", [128, 512], BF16) for i in range(2)])
        QR = Ring([A.alloc(f"QR{i}", [128, 512], BF16) for i in range(2)])
        PT = Ring([A.alloc(f"PT{i}", [128, 512], BF16) for i in range(6)])
        rt = Ring([A.alloc(f"rt{i}", [128, 512], F32) for i in range(4)])
        rcp = Ring([A.alloc(f"rcp{i}", [128, 512], F32) for i in range(2)])
        accP = Ring([A.alloc(f"accP{i}", [128, 512], F32) for i in range(2)])
        accD = Ring([A.alloc(f"accD{i}", [128, 512], F32) for i in range(2)])
        self.phaseB_top = A.cur
        scale = float(192 ** -0.5)
        def kv_proj(h):
            kt_b = KT.next()
            vh_b = VH.next()
            for tg in range(NTG):
                tsl = slice(tg * 512, (tg + 1) * 512)
                ps = self.ps_gen.next()
                self.mm(ps.t[:], ukv_v[:, 0, h * 256:h * 256 + 128], ckvn.t[:, tsl], True, True, [ukv_tok, ckvn.toks[tg]], [ps.tok])
                S.op("dve", lambda e: e.tensor_copy(kt_b.t[:, tsl], ps.t[:]), reads=[ps.tok], writes=[kt_b.toks[tg]])
                ps = self.ps_gen.next()
                for j in range(4):
                    kt = tg * 4 + j
                    self.mm(ps.t[:, j * 128:(j + 1) * 128], ckvn.t[:, kt * 128:(kt + 1) * 128], ukv_v[:, 0, h * 256 + 128:h * 256 + 256],
                            True, True, [ukv_tok, ckvn.toks[tg]], [ps.tok])
                S.op("act", lambda e: e.copy(
                    vh_b.t[:, tg * 4:(tg + 1) * 4, :].rearrange("p a b -> p (a b)"), ps.t[:]), reads=[ps.tok], writes=[vh_b.toks[tg]])
            return kt_b, vh_b

        def q_proj(h, qg):
            qsl = slice(qg * 512, (qg + 1) * 512)
            qn, qr = QN.next(), QR.next()
            ps = self.ps_gen.next()
            for k3 in range(3):
                self.mm(ps.t[:], uq_v[:, k3, h * 256:h * 256 + 128], cqn.t[:, k3, qsl], k3 == 0, k3 == 2,
                        [uq_tok, cqn.toks[k3 * NTG + qg]], [ps.tok])
            S.op("act", lambda e: e.copy(qn.t[:], ps.t[:]), reads=[ps.tok], writes=[qn.tok])
            psa = self.ps_gen.next()
            for k3 in range(3):
                self.mm(psa.t[0:64, :], uq_v[:, k3, h * 256 + 128:h * 256 + 192], cqn.t[:, k3, qsl], k3 == 0, k3 == 2,
                        [uq_tok, cqn.toks[k3 * NTG + qg]], [psa.tok])
            psb = self.ps_gen.next()
            for k3 in range(3):
                self.mm(psb.t[0:64, :], uq_v[:, k3, h * 256 + 192:h * 256 + 256], cqn.t[:, k3, qsl], k3 == 0, k3 == 2,
                        [uq_tok, cqn.toks[k3 * NTG + qg]], [psb.tok])
            ta, tb = rt.next(), rt.next()
            S.op("dve", lambda e: e.tensor_tensor(ta.t[0:64, :], psa.t[0:64, :], self.cs64.t[0:64, qsl], ALU.mult),
                 reads=[psa.tok, self.cs64.tok], writes=[ta.tok])
            S.op("dve", lambda e: e.tensor_tensor(tb.t[0:64, :], psb.t[0:64, :], self.sn64.t[0:64, qsl], ALU.mult),
                 reads=[psb.tok, self.sn64.tok], writes=[tb.tok])
            S.op("pool", lambda e: e.tensor_tensor(qr.t[0:64, :], ta.t[0:64, :], tb.t[0:64, :], ALU.add),
                 reads=[ta.tok, tb.tok], writes=[qr.tok])
            return qn, qr

        def attend(h, qg, kt_b, vh_b, qn, qr, acc_o, acc_s, prev_fin):
            qsl = slice(qg * 512, (qg + 1) * 512)
            ap_, ad_ = accP.next(), accD.next()

            def score(kt):
                ps = self.ps_gen.next()
                ksl = slice(kt * 128, (kt + 1) * 128)
                self.mm(ps.t[:], kt_b.t[:, ksl], qn.t[:], True, False, [kt_b.toks[kt // 4], qn.tok], [ps.tok])
                self.mm(ps.t[:], krope.t[0:64, ksl], qr.t[0:64, :], False, True, [krope.toks[kt // 4], qr.tok], [ps.tok])
                p = PT.next()
                S.op("act", lambda e: e.activation(p.t[:], ps.t[:], AF.Exp, scale=scale), reads=[ps.tok], writes=[p.tok])
                return p

            def pv(kt, p):
                self.mm(acc_o.t[:], vh_b.t[:, kt, :], p.t[:], kt == 0, kt == NKT - 1, [vh_b.toks[kt // 4], p.tok], [acc_o.tok])
                if kt % 4 == 3:
                    self.mm(acc_s.t[:], self.ones.t[:], p.t[:], kt == 3, False, [self.ones.tok, p.tok], [acc_s.tok])
                elif kt == 0:
                    S.op("dve", lambda e: e.tensor_copy(ad_.t[:], p.t[:]), reads=[p.tok], writes=[ad_.tok])
                else:
                    S.op("dve", lambda e: e.tensor_tensor(ad_.t[:], ad_.t[:], p.t[:], ALU.add), reads=[ad_.tok, p.tok], writes=[ad_.tok])

            LA = 2
            pend = [score(k) for k in range(min(LA, NKT))]
            for kt in range(NKT):
                if kt + LA < NKT:
                    pend.append(score(kt + LA))
                pv(kt, pend.pop(0))
                if kt == 1 and prev_fin is not None:
                    prev_fin()

            def fin():
                self.mm(acc_s.t[:], self.ones_f.t[:], ad_.t[:], NKT < 4, True, [self.ones_f.tok, ad_.tok], [acc_s.tok])
                rc = rcp.next()
                S.op("dve", lambda e: e.reciprocal(rc.t[:], acc_s.t[:]), reads=[acc_s.tok], writes=[rc.tok])
                tn = rt.next()
                S.op("dve", lambda e: e.tensor_tensor(tn.t[:], acc_o.t[:], rc.t[:], ALU.mult),
                     reads=[acc_o.tok, rc.tok], writes=[tn.tok])
                S.op("pool", lambda e: e.tensor_tensor(ym.t[:, h, qsl], tn.t[:], ym.t[:, h, qsl], ALU.mult),
                     reads=[tn.tok, ymtok(h, qg)], writes=[ymtok(h, qg)])
            return fin

        units = [(h, qg) for h in range(8) for qg in range(NTG)]
        kvs = {0: kv_proj(0)}
        qs = {units[0]: q_proj(*units[0])}
        fin = None
        for i, (h, qg) in enumerate(units):
            if i + 1 < len(units):
                nh, nq = units[i + 1]
                if nh not in kvs:
                    kvs[nh] = kv_proj(nh)
                qs[(nh, nq)] = q_proj(nh, nq)
            kt_b, vh_b = kvs[h]
            qn, qr = qs.pop((h, qg))
            fin = attend(h, qg, kt_b, vh_b, qn, qr, self.ps_acc[(i % 2) * 2], self.ps_acc[(i % 2) * 2 + 1], fin)
        fin()
        nb = self.next_blocks(nxt)
        self.prefetch(nb[0], nb[1])
        for hf in range(2):
            wv, wtok = self.load_w(f"woa{hf}_{li}")
            ast = self.acc_begin(s, [(hf * 4 + o4, tg) for tg in range(NTG) for o4 in range(4)], accst)
            for tg in range(NTG):
                for o4 in range(4):
                    ps = self.ps_gen.next()
                    for kc in range(8):
                        self.mm(ps.t[:], wv[:, kc, o4 * 128:(o4 + 1) * 128], ym.t[:, kc, tg * 512:(tg + 1) * 512],
                                kc == 0, kc == 7, [wtok, ymtok(kc, tg)], [ps.tok])
                    self.acc_step(ast, ps)
            if hf == 0:
                self.prefetch(nb[2])
        A.release(m_layer)

    def odd_layer(self, s, li, nxt=None):
        S, A, T, NTG, NKT = self.S, self.A, self.T, self.NTG, self.NKT
        m_layer = A.mark()
        self.make_h(s, "r_norm_g", li)
        self.rope_tables(s, 256)
        hT, htok = self.hT, self.htok
        accst = Ring([A.alloc(f"acs{i}", [128, 512], F32) for i in range(4)])
        KTb = A.alloc("rKT", [128, 2, T], BF16, ntok=2 * NTG)
        VHb = A.alloc("rVH", [128, NKT, 512], BF16, ntok=NKT)
        Gr = Ring([A.alloc(f"rG{i}", [128, self.GW], BF16) for i in range(2)])
        gbs = {}

        def load_g(h):
            if h in gbs or h > 3:
                return
            gb = Gr.next()
            S.dma("sp", gb.t[:], self.gtab[li, h], reads=[self.gtab_tok[li][h]], writes=[gb.tok], key=("gld", gb.off))
            gbs[h] = gb
        QT = Ring([A.alloc(f"rQT{i}", [128, 2, 512], BF16, ntok=2) for i in range(2)])
        SG = Ring([A.alloc(f"rSG{i}", [128, 4, 512], BF16, ntok=4) for i in range(2)])
        PT = Ring([A.alloc(f"rPT{i}", [128, 512], BF16) for i in range(4)])
        rt = Ring([A.alloc(f"rrt{i}", [128, 512], F32) for i in range(6)])
        obf = A.alloc("obf", [128, 4, 512], BF16, ntok=4)
        osq = A.alloc("osq", [128, 4, 512], BF16, ntok=4)
        st = [A.alloc(f"gst{i}", [128, 512], F32) for i in range(5)]
        ymr = Ring([A.alloc(f"rym{i}", [128, 4, 512], BF16, ntok=4) for i in range(1)])

        def rope_pair(ps1, ps2, sl, out1, out2, o1tok, o2tok):
            a, b, c, d = rt.next(), rt.next(), rt.next(), rt.next()
            S.op("dve", lambda e: e.tensor_tensor(a.t[:], ps1.t[:], self.cs256.t[:, sl], ALU.mult), reads=[ps1.tok, self.cs256.tok], writes=[a.tok])
            S.op("dve", lambda e: e.tensor_tensor(b.t[:], ps2.t[:], self.sn256.t[:, sl], ALU.mult), reads=[ps2.tok, self.sn256.tok], writes=[b.tok])
            S.op("dve", lambda e: e.tensor_tensor(c.t[:], ps2.t[:], self.cs256.t[:, sl], ALU.mult), reads=[ps2.tok, self.cs256.tok], writes=[c.tok])
            S.op("dve", lambda e: e.tensor_tensor(d.t[:], ps1.t[:], self.sn256.t[:, sl], ALU.mult), reads=[ps1.tok, self.sn256.tok], writes=[d.tok])
            S.op("dve", lambda e: e.tensor_tensor(out1, a.t[:], b.t[:], ALU.subtract), reads=[a.tok, b.tok], writes=[o1tok])
            S.op("dve", lambda e: e.tensor_tensor(out2, c.t[:], d.t[:], ALU.add), reads=[c.tok, d.tok], writes=[o2tok])

        wts = {}

        def pre(h):
            load_g(h)
            wv, wtok = self.load_w(f"rk_{li}_{h}")
            for tg in range(NTG):
                tsl = slice(tg * 512, (tg + 1) * 512)
                pp = []
                for j in range(2):
                    ps = self.ps_gen.next()
                    for kc in range(KC):
                        self.mm(ps.t[:], wv[:, kc, j * 128:(j + 1) * 128], hT.t[:, kc, tsl], kc == 0, kc == KC - 1, [wtok, htok(kc, tg)], [ps.tok])
                    pp.append(ps)
                rope_pair(pp[0], pp[1], tsl, KTb.t[:, 0, tsl], KTb.t[:, 1, tsl], KTb.toks[tg], KTb.toks[NTG + tg])
            for kt in range(NKT):
                ps = self.ps_gen.next()
                for kc in range(KC):
                    self.mm(ps.t[:], hT.t[:, kc, kt * 128:(kt + 1) * 128], wv[:, kc, 256:768], kc == 0, kc == KC - 1, [wtok, htok(kc, kt // 4)], [ps.tok])
                S.op("act", lambda e: e.copy(VHb.t[:, kt, :], ps.t[:]), reads=[ps.tok], writes=[VHb.toks[kt]])
            wts[h] = (self.load_w(f"rq_{li}_{h}"), self.load_w(f"rwo_{li}_{h}"))

        def stage_a(h, qg):
            (qv, qtok), _ = wts[h]
            qsl = slice(qg * 512, (qg + 1) * 512)
            qt = QT.next()
            pp = []
            for j in range(2):
                ps = self.ps_gen.next()
                for kc in range(KC):
                    self.mm(ps.t[:], qv[:, kc, j * 128:(j + 1) * 128], hT.t[:, kc, qsl], kc == 0, kc == KC - 1, [qtok, htok(kc, qg)], [ps.tok])
                pp.append(ps)
            rope_pair(pp[0], pp[1], qsl, qt.t[:, 0, :], qt.t[:, 1, :], qt.toks[0], qt.toks[1])
            sgb = SG.next()
            for j in range(4):
                ps = self.ps_gen.next()
                for kc in range(KC):
                    self.mm(ps.t[:], qv[:, kc, 256 + j * 128:256 + (j + 1) * 128], hT.t[:, kc, qsl], kc == 0, kc == KC - 1, [qtok, htok(kc, qg)], [ps.tok])
                S.op("act", lambda e: e.activation(sgb.t[:, j, :], ps.t[:], AF.Silu), reads=[ps.tok], writes=[sgb.toks[j]])
                g = self.vcol("r_gn_g", li * 16 + h * 4 + j)
                S.op("dve", lambda e: e.tensor_scalar(sgb.t[:, j, :], sgb.t[:, j, :], g, None, ALU.mult),
                     reads=[sgb.toks[j], self.vecs.tok], writes=[sgb.toks[j]])
            return qt, sgb

        def stage_b(h, qg, qt):
            Gb = gbs[h]

            def score(kt):
                ps = self.ps_gen.next()
                ksl = slice(kt * 128, (kt + 1) * 128)
                for j in range(2):
                    self.mm(ps.t[:], KTb.t[:, j, ksl], qt.t[:, j, :], j == 0, j == 1, [KTb.toks[j * NTG + kt // 4], qt.toks[j]], [ps.tok])
                p = PT.next()
                c0 = qg * 512 - kt * 128 + T - 128
                S.op("dve", lambda e: e.tensor_tensor(p.t[:], ps.t[:], Gb.t[:, c0:c0 + 512], ALU.mult),
                     reads=[ps.tok, Gb.tok], writes=[p.tok])
                return p

            def pv(kt, p):
                for jv in range(4):
                    acc = self.ps_acc[jv]
                    self.mm(acc.t[:], VHb.t[:, kt, jv * 128:(jv + 1) * 128], p.t[:], kt == 0, kt == NKT - 1, [VHb.toks[kt], p.tok], [acc.tok])

            LA = 3
            pend = [score(k) for k in range(min(LA, NKT))]
            for kt in range(NKT):
                if kt + LA < NKT:
                    pend.append(score(kt + LA))
                pv(kt, pend.pop(0))
            for jv in range(4):
                acc = self.ps_acc[jv]
                S.op("act", lambda e: e.copy(obf.t[:, jv, :], acc.t[:]), reads=[acc.tok], writes=[obf.toks[jv]])
                S.op("act", lambda e: e.activation(osq.t[:, jv, :], obf.t[:, jv, :], AF.Square), reads=[obf.toks[jv]], writes=[osq.toks[jv]])

        def stage_d(h, qg, sgb):
            ps1 = self.ps_gen.next()
            for jv in range(4):
                self.mm(ps1.t[:], self.ones_v.t[:], obf.t[:, jv, :], jv == 0, jv == 3, [self.ones_v.tok, obf.toks[jv]], [ps1.tok])
            ps2 = self.ps_gen.next()
            for jv in range(4):
                self.mm(ps2.t[:], self.ones_v.t[:], osq.t[:, jv, :], jv == 0, jv == 3, [self.ones_v.tok, osq.toks[jv]], [ps2.tok])
            mean, msq, var, lt_, rstd = st
            S.op("act", lambda e: e.copy(mean.t[:], ps1.t[:]), reads=[ps1.tok], writes=[mean.tok])
            S.op("act", lambda e: e.activation(msq.t[:], mean.t[:], AF.Square), reads=[mean.tok], writes=[msq.tok])
            S.op("dve", lambda e: e.tensor_tensor(var.t[:], ps2.t[:], msq.t[:], ALU.subtract), reads=[ps2.tok, msq.tok], writes=[var.tok])
            S.op("dve", lambda e: e.tensor_scalar(var.t[:], var.t[:], 0.0, None, ALU.max), reads=[var.tok], writes=[var.tok])
            S.op("act", lambda e: e.activation(lt_.t[:], var.t[:], AF.Ln, bias=EPS, scale=1.0), reads=[var.tok], writes=[lt_.tok])
            S.op("act", lambda e: e.activation(rstd.t[:], lt_.t[:], AF.Exp, scale=-0.5), reads=[lt_.tok], writes=[rstd.tok])
            ymb = ymr.next()
            for jv in range(4):
                a, b = rt.next(), rt.next()
                S.op("pool", lambda e: e.tensor_tensor(a.t[:], obf.t[:, jv, :], mean.t[:], ALU.subtract),
                     reads=[obf.toks[jv], mean.tok], writes=[a.tok])
                S.op("pool", lambda e: e.tensor_tensor(b.t[:], a.t[:], rstd.t[:], ALU.mult),
                     reads=[a.tok, rstd.tok], writes=[b.tok])
                S.op("pool", lambda e: e.tensor_tensor(ymb.t[:, jv, :], b.t[:], sgb.t[:, jv, :], ALU.mult),
                     reads=[b.tok, sgb.toks[jv]], writes=[ymb.toks[jv]])
            return ymb

        def stage_f(h, qg, ymb, ast):
            _, (ov, otok) = wts[h]
            for oc in range(8):
                ps = self.ps_gen.next()
                for kc in range(4):
                    self.mm(ps.t[:], ov[:, kc, oc * 128:(oc + 1) * 128], ymb.t[:, kc, :], kc == 0, kc == 3, [otok, ymb.toks[kc]], [ps.tok])
                self.acc_step(ast, ps)

        units = [(h, qg) for h in range(4) for qg in range(NTG)]
        prev = None
        nb = self.next_blocks(nxt)

        def upcoming(h):
            if h < 3:
                return (f"rk_{li}_{h + 1}", f"rq_{li}_{h + 1}", f"rwo_{li}_{h + 1}")
            return nb

        def acc_tiles(qg):
            return self.acc_begin(s, [(oc, qg) for oc in range(8)], accst)

        for i, (h, qg) in enumerate(units):
            if qg == 0:
                pre(h)
            qt, sgb = stage_a(h, qg)
            if qg == NTG - 1 and NTG >= 2:
                self.prefetch(upcoming(h)[2])
            if prev is not None:
                prev["ymb"] = stage_d(prev["h"], prev["qg"], prev["sgb"])
                prev["ast"] = acc_tiles(prev["qg"])
            stage_b(h, qg, qt)
            if prev is not None:
                stage_f(prev["h"], prev["qg"], prev["ymb"], prev["ast"])
            if qg == min(1, NTG - 2) and NTG >= 2:
                up = upcoming(h)
                self.prefetch(up[0], up[1])
                load_g(h + 1)
            prev = {"h": h, "qg": qg, "sgb": sgb}
        prev["ymb"] = stage_d(prev["h"], prev["qg"], prev["sgb"])
        prev["ast"] = acc_tiles(prev["qg"])
        stage_f(prev["h"], prev["qg"], prev["ymb"], prev["ast"])
        A.release(m_layer)

    def store_out(self, s):
        S, A, T = self.S, self.A, self.T
        m0 = A.mark()
        xl = Ring([A.alloc(f"fxl{i}", [128, KC, 512], F32) for i in range(2)])
        sq = Ring([A.alloc(f"fsq{i}", [128, 512], BF16) for i in range(3)])
        tmp = Ring([A.alloc(f"flt{i}", [128, 512], F32) for i in range(2)])
        rs = Ring([A.alloc(f"frs{i}", [128, 512], F32) for i in range(2)])
        yb = Ring([A.alloc(f"fy{i}", [128, KC, 512], F32) for i in range(2)])
        ost = Ring([A.alloc(f"fo{i}", [128, D], F32) for i in range(3)])
        for tg in range(self.NTG):
            xb = xl.next()
            src = self.xT[s][:, :, tg * 512:(tg + 1) * 512].rearrange("c p t -> p c t")
            S.dma("sp", xb.t[:], src, reads=[self.xT_tok[s][c][tg] for c in range(KC)], writes=[xb.tok], key=("fxl", xb.off))
            if self.final_norm:
                ps = self.ps_gen.next()
                for c in range(KC):
                    q = sq.next()
                    S.op("act", lambda e, q=q, xb=xb, c=c: e.activation(q.t[:], xb.t[:, c, :], AF.Square), reads=[xb.tok], writes=[q.tok])
                    self.mm(ps.t[:], self.ones_d.t[:], q.t[:], c == 0, c == KC - 1, [self.ones_d.tok, q.tok], [ps.tok])
                t_, r_ = tmp.next(), rs.next()
                self.rstd_from_ps(ps, 1.0, t_, r_)
                y = yb.next()
                for c in range(KC):
                    g = self.vcol("final_g", c)
                    S.op("dve", lambda e, xb=xb, c=c, g=g, r_=r_, y=y: e.scalar_tensor_tensor(
                        y.t[:, c, :], xb.t[:, c, :], g, r_.t[:], ALU.mult, ALU.mult),
                        reads=[xb.tok, r_.tok, self.vecs.tok], writes=[y.tok])
            else:
                y = xb
            for j in range(4):
                ob = ost.next()
                for half in range(2):
                    ps = self.ps_gen.next()
                    for cc in range(4):
                        c = half * 4 + cc
                        S.op("pe", lambda e, ps=ps, y=y, j=j, c=c, cc=cc: e.transpose(ps.t[:, cc * 128:(cc + 1) * 128], y.t[:, c, j * 128:(j + 1) * 128], self.ident.t[:]),
                             reads=[y.tok, self.ident.tok], writes=[ps.tok])
                    if half == 0:
                        S.op("act", lambda e, ob=ob, ps=ps: e.copy(ob.t[:, 0:512], ps.t[:]), reads=[ps.tok], writes=[ob.tok])
                    else:
                        S.op("dve", lambda e, ob=ob, ps=ps: e.tensor_copy(ob.t[:, 512:1024], ps.t[:]), reads=[ps.tok, ob.tok], writes=[ob.tok])
                t0 = tg * 512 + j * 128
                S.dma("sp", self.out[s, t0:t0 + 128, :], ob.t[:], reads=[ob.tok], writes=[self.out_tok], key=("fo", ob.off))
            yield tg


_PROG_CACHE = {}


def _get_prog(T, nseq, layers, final_norm, debug=False):
    k = (T, nseq, tuple(layers), final_norm, debug)
    if k not in _PROG_CACHE:
        _PROG_CACHE[k] = Prog(T, nseq, list(layers), final_norm, debug=debug)
    return _PROG_CACHE[k]


def _dtab(T):
    GW = 2 * T - 128
    c = np.arange(GW, dtype=np.float32)[None, :]
    i = np.arange(128, dtype=np.float32)[:, None]
    return np.ascontiguousarray(c - i - np.float32(T - 128)).astype(np.float32)


def run_layers(inp, layers, final_norm, x=None, n_cores=N_CORES, debug=False):
    x = np.asarray(inp["x"] if x is None else x, np.float32)
    B, T, _ = x.shape
    nseq = B // n_cores
    prog = _get_prog(T, nseq, layers, final_norm, debug)
    vecs = _build_vecs(inp, T)
    ident = np.eye(128, dtype=np.float32)
    dtab = _dtab(T)
    pos = np.asarray(inp["positions"], np.int32)
    shared = {k: np.ascontiguousarray(np.asarray(inp[k], np.float32)) for k in
              ("a_w_in", "a_w_uq", "a_w_ukv", "a_pool_w", "a_w_out", "r_w_in", "r_w_out")}
    in_maps = []
    for c in range(n_cores):
        m = dict(shared)
        m["x"] = np.ascontiguousarray(x[c * nseq:(c + 1) * nseq])
        m["pos"] = np.ascontiguousarray(pos[c * nseq:(c + 1) * nseq])
        m["vecs"] = vecs
        m["ident"] = ident
        m["dtab"] = dtab
        in_maps.append(m)
    res = run_bass_kernel_spmd(prog.nc, in_maps, core_ids=list(range(n_cores)))
    if debug:
        return res.results
    return np.concatenate([r["out"] for r in res.results], axis=0)


FUSED = True


def kernel(**inputs):
    if FUSED:
        return run_layers(inputs, [0, 1, 2, 3], True)
    x = None
    for l in range(4):
        x = run_layers(inputs, [l], l == 3, x=x)
    return x
```

```python
import numpy as np
import concourse.bass as bass
import concourse.mybir as mybir
from concourse.bass_utils import run_bass_kernel_spmd

F32 = mybir.dt.float32
BF16 = mybir.dt.bfloat16
I32 = mybir.dt.int32
AF = mybir.ActivationFunctionType
ALU = mybir.AluOpType

D = 1024
KC = 8
EPS = 1e-6
N_CORES = 8


class Tok:
    __slots__ = ("lastw", "readers", "name")

    def __init__(self, name=""):
        self.lastw = None
        self.readers = {}
        self.name = name


class _Op:
    __slots__ = ("fn", "waits", "signal", "dkey")


ENGS = ("pe", "act", "dve", "pool", "sp")


class _Rec:
    def __getattr__(self, name):
        def f(*a, **k):
            self.__dict__["call"] = (name, a, k)
            return None
        return f


class Sched:
    def __init__(self, nc):
        self.nc = nc
        self.streams = {e: [] for e in ENGS}
        self.waited = {e: {} for e in ENGS}
        self.dcount = {}
        self.dorder = []

    def _deps(self, eng, reads, writes):
        deps = {}

        def add(k, v):
            if deps.get(k, -1) < v:
                deps[k] = v

        for t in reads:
            if t.lastw is not None:
                add(*t.lastw)
        for t in writes:
            if t.lastw is not None:
                add(*t.lastw)
            for k, v in t.readers.items():
                add(k, v)
        if eng == "pe":
            deps.pop(("e", "pe"), None)
        w = self.waited[eng]
        out = []
        for k, v in deps.items():
            if w.get(k, -1) >= v:
                continue
            w[k] = v
            out.append((k, v))
            if k[0] == "e":
                self.streams[k[1]][v].signal = True
        return out

    def _mark(self, ref, reads, writes):
        k, v = ref
        ws = set(id(t) for t in writes)
        for t in writes:
            t.lastw = ref
            t.readers = {}
        for t in reads:
            if id(t) not in ws:
                if t.readers.get(k, -1) < v:
                    t.readers[k] = v

    def op(self, eng, fn, reads=(), writes=()):
        o = _Op()
        rec = _Rec()
        fn(rec)
        name, a, k = rec.call
        o.fn = lambda e, name=name, a=a, k=k: getattr(e, name)(*a, **k)
        o.signal = False
        o.dkey = None
        o.waits = self._deps(eng, reads, writes)
        idx = len(self.streams[eng])
        self.streams[eng].append(o)
        self._mark((("e", eng), idx), reads, writes)
        return o

    def dma(self, queue, out, in_, reads=(), writes=(), key=None, **kw):
        o = _Op()
        o.signal = False
        o.dkey = key
        o.waits = self._deps(queue, reads, writes)
        if key not in self.dcount:
            self.dcount[key] = 0
            self.dorder.append(key)
        self.dcount[key] += 16
        o.fn = lambda e: e.dma_start(out=out, in_=in_, **kw)
        self.streams[queue].append(o)
        self._mark((("d", key), self.dcount[key]), reads, writes)
        return o

    def emit(self):
        nc = self.nc
        fin = []
        for e in ENGS:
            if e != "sp":
                idxs = [i for i, o in enumerate(self.streams[e]) if o.dkey is None]
                if idxs:
                    self.streams[e][idxs[-1]].signal = True
                    fin.append((("e", e), idxs[-1]))
        for k in self.dorder:
            fin.append((("d", k), self.dcount[k]))
        o = _Op()
        o.fn = None
        o.signal = False
        o.dkey = None
        o.waits = [(k, v) for k, v in fin if self.waited["sp"].get(k, -1) < v]
        self.streams["sp"].append(o)

        esem = {}
        cum = {}
        for e in ENGS:
            if e == "sp":
                continue
            esem[e] = nc.alloc_semaphore(name=f"s_{e}")
            c = 0
            arr = []
            for o in self.streams[e]:
                if o.signal:
                    c += 1
                arr.append(c)
            cum[e] = arr
        dsem = {}
        for i, k in enumerate(self.dorder):
            dsem[k] = nc.alloc_semaphore(name=f"d_{i}")
        self.n_sems = len(esem) + len(dsem)

        def run(ename, eng):
            for o in self.streams[ename]:
                for k, v in o.waits:
                    if k[0] == "e":
                        eng.wait_ge(esem[k[1]], cum[k[1]][v])
                    else:
                        eng.wait_ge(dsem[k[1]], v)
                if o.fn is None:
                    continue
                inst = o.fn(eng)
                if o.dkey is not None:
                    inst.then_inc(dsem[o.dkey], 16)
                elif o.signal:
                    inst.then_inc(esem[ename], 1)

        with nc.Block() as block:

            @block.tensor
            def _(eng):
                run("pe", eng)

            @block.scalar
            def _(eng):
                run("act", eng)

            @block.vector
            def _(eng):
                run("dve", eng)

            @block.gpsimd
            def _(eng):
                run("pool", eng)

            @block.sync
            def _(eng):
                run("sp", eng)

    def stats(self):
        return {e: len(s) for e, s in self.streams.items()}


class Buf:
    def __init__(self, t, toks, off=0):
        self.t = t
        self.toks = toks
        self.off = off

    @property
    def tok(self):
        return self.toks[0]


class Arena:
    def __init__(self, nc, base, limit):
        self.nc = nc
        self.cur = base
        self.limit = limit
        self.records = []
        self.n = 0
        self.peak = base

    def alloc(self, name, shape, dtype, ntok=1):
        nbytes = int(np.prod(shape[1:])) * mybir.dt.size(dtype)
        nbytes = (nbytes + 31) // 32 * 32
        off = self.cur
        assert off + nbytes <= self.limit, f"SBUF overflow allocating {name}: {off}+{nbytes} > {self.limit}"
        self.n += 1
        t = self.nc.alloc_sbuf_tensor_at(f"{name}_{self.n}", list(shape), dtype, offset=off)
        self.cur = off + nbytes
        self.peak = max(self.peak, self.cur)
        toks = [Tok(f"{name}{i}") for i in range(ntok)]
        inherit = {}
        for (s, e, otoks) in self.records:
            if s < off + nbytes and off < e:
                for ot in otoks:
                    if ot.lastw is not None:
                        k, v = ot.lastw
                        if inherit.get(k, -1) < v:
                            inherit[k] = v
                    for k, v in ot.readers.items():
                        if inherit.get(k, -1) < v:
                            inherit[k] = v
        for tk in toks:
            tk.readers = dict(inherit)
        self.records.append((off, off + nbytes, toks))
        return Buf(t, toks, off)

    def mark(self):
        return self.cur

    def release(self, m):
        self.cur = m


class Ring:
    def __init__(self, bufs):
        self.bufs = bufs
        self.i = 0

    def next(self):
        b = self.bufs[self.i % len(self.bufs)]
        self.i += 1
        return b


def _vec_layout():
    cols = {}
    c = 0

    def add(name, n):
        nonlocal c
        cols[name] = c
        c += n

    add("a_norm_g", 16)
    add("a_q_norm_g", 6)
    add("a_kv_norm_g", 2)
    add("a_pool_scale", 16)
    add("r_norm_g", 16)
    add("r_gn_g", 32)
    add("final_g", 8)
    add("decay", 16)
    add("inv64", 1)
    add("inv256", 1)
    add("poolcnt", 64)
    return cols, c


VCOL, NV = _vec_layout()
POOL_W = (2, 4, 8, 16)


def _build_vecs(inp, T):
    v = np.zeros((128, NV), np.float32)

    def put(name, arr, nchunk):
        a = np.asarray(arr, np.float32)
        L = a.shape[0]
        a = a.reshape(L, nchunk, 128).transpose(2, 0, 1).reshape(128, L * nchunk)
        v[:, VCOL[name]:VCOL[name] + L * nchunk] = a

    put("a_norm_g", inp["a_norm_g"], 8)
    put("a_q_norm_g", inp["a_q_norm_g"], 3)
    put("a_kv_norm_g", inp["a_kv_norm_g"], 1)
    put("a_pool_scale", inp["a_pool_scale"], 8)
    put("r_norm_g", inp["r_norm_g"], 8)
    put("r_gn_g", inp["r_gn_g"], 16)
    put("final_g", np.asarray(inp["final_norm_g"])[None, :], 8)
    dec = np.stack([np.asarray(inp["r_decay_fwd"], np.float32), np.asarray(inp["r_decay_bwd"], np.float32)], axis=1)
    v[:, VCOL["decay"]:VCOL["decay"] + 16] = np.broadcast_to(dec.reshape(1, 16), (128, 16))
    inv64 = (1.0 / (10000.0 ** (np.arange(0, 64, 2, dtype=np.float32) / np.float32(64)))).astype(np.float32)
    inv256 = (1.0 / (10000.0 ** (np.arange(0, 256, 2, dtype=np.float32) / np.float32(256)))).astype(np.float32)
    v[:, VCOL["inv64"]] = inv64[np.arange(128) % 32]
    v[:, VCOL["inv256"]] = inv256
    for g, w in enumerate(POOL_W):
        left = w // 2
        right = w - 1 - left
        t = np.arange(T)
        lo = np.clip(t - left, 0, T - 1)
        hi = np.clip(t + right, 0, T - 1)
        cnt = (hi - lo + 1).astype(np.float32)
        e = np.ones(16, np.float32)
        e[:8] = 1.0 / cnt[:8]
        e[8:] = 1.0 / cnt[T - 8:]
        v[:, VCOL["poolcnt"] + g * 16: VCOL["poolcnt"] + (g + 1) * 16] = e[None, :]
    return v


class Prog:
    def __init__(self, T, nseq, layers, final_norm, debug=False):
        self.T = T
        self.nseq = nseq
        self.layers = layers
        self.final_norm = final_norm
        self.NTG = T // 512
        self.NKT = T // 128
        self.GW = 2 * T - 128
        self.debug = debug
        nc = bass.Bass("TRN2", target_bir_lowering=False)
        self.nc = nc
        self.S = Sched(nc)
        self.pref = {}
        self._declare_dram()
        self.A = Arena(nc, 24576, 229376)
        self._alloc_static()
        self.prologue()
        for s in range(nseq):
            self.sequence(s)
        self.S.emit()

    def _declare_dram(self):
        nc, T, nseq = self.nc, self.T, self.nseq
        ein = lambda name, shape, dt=F32: nc.dram_tensor(name, list(shape), dt, kind="ExternalInput").ap()
        self.x_in = ein("x", [nseq, T, D])
        self.pos_in = ein("pos", [nseq, T], I32)
        self.vecs_in = ein("vecs", [128, NV])
        self.ident_in = ein("ident", [128, 128])
        self.dtab_in = ein("dtab", [128, self.GW])
        self.a_w_in = ein("a_w_in", [2, D, 3648])
        self.a_w_uq = ein("a_w_uq", [2, 384, 1536])
        self.a_w_ukv = ein("a_w_ukv", [2, 128, 2048])
        self.a_pool_w = ein("a_pool_w", [2, 4, 256, 256])
        self.a_w_out = ein("a_w_out", [2, 2048, D])
        self.r_w_in = ein("r_w_in", [2, D, 6144])
        self.r_w_out = ein("r_w_out", [2, 2048, D])
        self.out = nc.dram_tensor("out", [nseq, T, D], F32, kind="ExternalOutput").ap()
        self.xT = [nc.dram_tensor(f"xT{s}", [KC, 128, T], F32).ap() for s in range(nseq)]
        self.xT_tok = [[[Tok(f"xT{s}_{c}_{g}") for g in range(self.NTG)] for c in range(KC)] for s in range(nseq)]
        self.wb = {}

        def blk(name, kc, ncols):
            ap = nc.dram_tensor("wb_" + name, [128, kc * ncols], BF16).ap()
            self.wb[name] = (ap, Tok("wb_" + name), kc, ncols)

        for l in range(2):
            for g in range(4):
                blk(f"e1_{l}_{g}", 8, 512)
            blk(f"pw_{l}", 8, 256)
            for hf in range(2):
                blk(f"wob{hf}_{l}", 8, 512)
                blk(f"ga{hf}_{l}", 8, 512)
                blk(f"woa{hf}_{l}", 8, 512)
            blk(f"e2_{l}", 8, 640)
            blk(f"uq_{l}", 3, 2048)
            blk(f"ukv_{l}", 1, 2048)
            for h in range(4):
                blk(f"rk_{l}_{h}", 8, 768)
                blk(f"rq_{l}_{h}", 8, 768)
                blk(f"rwo_{l}_{h}", 4, 1024)
        self.gtab = nc.dram_tensor("gtab", [2, 4, 128, self.GW], BF16).ap()
        self.gtab_tok = [[Tok(f"g{l}{h}") for h in range(4)] for l in range(2)]
        self.x_in_tok = Tok("x_in")
        self.out_tok = Tok("out")
        self.const_tok = Tok("const_in")

    def _alloc_static(self):
        A, nc, S = self.A, self.nc, self.S
        self.vecs = A.alloc("vecs", [128, NV], F32)
        self.ident = A.alloc("ident", [128, 128], F32)
        self.ones = A.alloc("ones", [128, 128], BF16)
        self.ones_d = A.alloc("ones_d", [128, 128], BF16)
        self.ones_v = A.alloc("ones_v", [128, 128], BF16)
        self.ldec = A.alloc("ldec", [128, 16], F32)
        self.ones_f = A.alloc("ones_f", [128, 128], F32)
        S.dma("sp", self.vecs.t[:], self.vecs_in, writes=[self.vecs.tok], key="const")
        S.dma("sp", self.ident.t[:], self.ident_in, writes=[self.ident.tok], key="const")
        self.vecs.tok.lastw = self.ident.tok.lastw = (("d", "const"), S.dcount["const"])
        S.op("pool", lambda e: e.memset(self.ones.t[:], 1.0), writes=[self.ones.tok])
        S.op("pool", lambda e: e.memset(self.ones_d.t[:], 1.0 / 1024.0), writes=[self.ones_d.tok])
        S.op("pool", lambda e: e.memset(self.ones_v.t[:], 1.0 / 512.0), writes=[self.ones_v.tok])
        S.op("pool", lambda e: e.memset(self.ones_f.t[:], 1.0), writes=[self.ones_f.tok])
        self.ps = []
        for i in range(8):
            t = nc.alloc_psum_tensor(f"ps{i}", [128, 512], F32)
            self.ps.append(Buf(t, [Tok(f"ps{i}")]))
        self.ps_gen = Ring(self.ps[0:4])
        self.ps_acc = self.ps[4:8]
        self.wring = Ring([A.alloc(f"wr{i}", [128, 6144], BF16) for i in range(4)])
        self.static_mark = A.mark()

    def dump(self, name, ap, toks):
        if not self.debug:
            return
        shape = list(ap.shape)
        d = self.nc.dram_tensor("dbg_" + name, shape, ap.dtype, kind="ExternalOutput").ap()
        self.S.dma("sp", d, ap, reads=list(toks), key=("dbg", name))

    def vcol(self, name, i=0):
        c = VCOL[name] + i
        return self.vecs.t[:, c:c + 1]

    def mm(self, out, lhsT, rhs, start, stop, reads, writes):
        self.S.op("pe", lambda e: e.matmul(out, lhsT, rhs, start=start, stop=stop), reads=reads, writes=writes)

    def prefetch(self, *names):
        for name in names:
            if name is not None and name in self.wb and name not in self.pref:
                self.pref[name] = self._load_w(name)

    def load_w(self, name):
        if name in self.pref:
            return self.pref.pop(name)
        return self._load_w(name)

    def next_blocks(self, nxt):
        if nxt is None:
            return (None, None, None)
        i = nxt // 2
        if nxt % 2 == 0:
            return (f"e1_{i}_0", f"pw_{i}", None)
        return (f"rk_{i}_0", f"rq_{i}_0", f"rwo_{i}_0")

    def _load_w(self, name):
        ap, tok, kc, ncols = self.wb[name]
        slot = self.wring.next()
        n = kc * ncols
        nsplit = max(1, n // 4096)
        step = n // nsplit
        for i in range(nsplit):
            self.S.dma("sp", slot.t[:, i * step:(i + 1) * step], ap[:, i * step:(i + 1) * step], reads=[tok], writes=[slot.tok], key=("wr", slot.off))
        view = slot.t[:, 0:kc * ncols].rearrange("p (k n) -> p k n", k=kc)
        return view, slot.tok

    def rstd_from_ps(self, ps, n_scale, tmp, out):
        S = self.S
        S.op("act", lambda e: e.activation(tmp.t[:], ps.t[:], AF.Ln, bias=EPS, scale=n_scale),
             reads=[ps.tok], writes=[tmp.tok])
        S.op("act", lambda e: e.activation(out.t[:], tmp.t[:], AF.Exp, scale=-0.5),
             reads=[tmp.tok], writes=[out.tok])

    def prologue(self):
        S, A, nc, T = self.S, self.A, self.nc, self.T
        m0 = A.mark()
        need_even = any(l % 2 == 0 for l in self.layers)
        need_odd = any(l % 2 == 1 for l in self.layers)
        ev = sorted(set(l // 2 for l in self.layers if l % 2 == 0))
        od = sorted(set(l // 2 for l in self.layers if l % 2 == 1))

        def cast(name, src3, key):
            ap, tok, kc, ncols = self.wb[name]
            dst = ap.rearrange("p (k n) -> p k n", k=kc)
            S.dma("pool", dst, src3, reads=[self.const_tok], writes=[tok], key=key)

        def rows(w2d, c0, n):
            return w2d[:, c0:c0 + n].rearrange("(k p) n -> p k n", p=128)

        def cast_even(i):
            key = ("cast", "e1", i)
            toks = []
            win = self.a_w_in[i]
            for g in range(4):
                ap, tok, kc, ncols = self.wb[f"e1_{i}_{g}"]
                dst = ap.rearrange("p (k n) -> p k n", k=8)
                S.dma("pool", dst[:, :, 0:256], rows(win, 576 + g * 256, 256), writes=[tok], key=key)
                S.dma("pool", dst[:, :, 256:512], rows(win, 576 + 1024 + 1024 + g * 256, 256), writes=[tok], key=key)
                toks.append(tok)
            ap, tok, kc, ncols = self.wb[f"pw_{i}"]
            S.dma("pool", ap.rearrange("p (k n) -> p k n", k=8),
                  self.a_pool_w[i].rearrange("g (k p) n -> p (g k) n", p=128), writes=[tok], key=key)
            toks.append(tok)
            for tk in toks:
                tk.lastw = (("d", key), S.dcount[key])
            key = ("cast", "e", i)
            toks = []
            for hf in range(2):
                cast(f"wob{hf}_{i}", self.a_w_out[i, 1024:2048, hf * 512:(hf + 1) * 512].rearrange("(k p) n -> p k n", p=128), key)
                cast(f"woa{hf}_{i}", self.a_w_out[i, 0:1024, hf * 512:(hf + 1) * 512].rearrange("(k p) n -> p k n", p=128), key)
                cast(f"ga{hf}_{i}", rows(win, 576 + 1024 + hf * 512, 512), key)
            cast(f"ukv_{i}", self.a_w_ukv[i].rearrange("(k p) n -> p k n", p=128), key)
            toks += [self.wb[f"{n}_{i}"][1] for n in ("wob0", "wob1", "woa0", "woa1", "ga0", "ga1", "ukv")]
            mm_ = A.mark()
            stg = A.alloc("stg_e2", [128, 8, 576], F32)
            ob = A.alloc("ob_e2", [128, 8, 640], BF16)
            S.dma("sp", stg.t[:], rows(win, 0, 576), writes=[stg.tok], key="pro_ld_e2")
            S.op("dve", lambda e, stg=stg, ob=ob: e.tensor_copy(ob.t[:, :, 0:576], stg.t[:]), reads=[stg.tok], writes=[ob.tok])
            S.op("act", lambda e, stg=stg, ob=ob: e.mul(ob.t[:, :, 576:608], stg.t[:, :, 544:576], -1.0), reads=[stg.tok], writes=[ob.tok])
            S.op("act", lambda e, stg=stg, ob=ob: e.copy(ob.t[:, :, 608:640], stg.t[:, :, 512:544]), reads=[stg.tok], writes=[ob.tok])
            ap, tok, kc, ncols = self.wb[f"e2_{i}"]
            S.dma("sp", ap.rearrange("p (k n) -> p k n", k=8), ob.t[:], reads=[ob.tok], writes=[tok], key="pro_st_e2")
            A.release(mm_)
            mm_ = A.mark()
            stg = A.alloc("stg_uq", [128, 3, 1536], F32)
            ob = A.alloc("ob_uq", [128, 3, 2048], BF16)
            S.dma("sp", stg.t[:], self.a_w_uq[i].rearrange("(k p) n -> p k n", p=128), writes=[stg.tok], key="pro_ld_uq")
            sv = stg.t[:].rearrange("p k (h c) -> p k h c", h=8)
            ov = ob.t[:].rearrange("p k (h c) -> p k h c", h=8)
            for k3 in range(3):
                S.op("dve", lambda e, k3=k3, sv=sv, ov=ov: e.tensor_copy(ov[:, k3, :, 0:192], sv[:, k3, :, :]), reads=[stg.tok], writes=[ob.tok])
                S.op("act", lambda e, k3=k3, sv=sv, ov=ov: e.mul(ov[:, k3, :, 192:224], sv[:, k3, :, 160:192], -1.0), reads=[stg.tok], writes=[ob.tok])
                S.op("act", lambda e, k3=k3, sv=sv, ov=ov: e.copy(ov[:, k3, :, 224:256], sv[:, k3, :, 128:160]), reads=[stg.tok], writes=[ob.tok])
            ap, tok, kc, ncols = self.wb[f"uq_{i}"]
            S.dma("sp", ap.rearrange("p (k n) -> p k n", k=3), ob.t[:], reads=[ob.tok], writes=[tok], key="pro_st_uq")
            A.release(mm_)
            for tk in toks:
                tk.lastw = (("d", key), S.dcount[key])
        def cast_odd(i):
            key = ("cast", "o", i)
            toks = []
            win = self.r_w_in[i]
            for h in range(4):
                ap, tok, kc, ncols = self.wb[f"rk_{i}_{h}"]
                dst = ap.rearrange("p (k n) -> p k n", k=8)
                S.dma("pool", dst[:, :, 0:256], rows(win, 1024 + h * 256, 256), writes=[tok], key=key)
                S.dma("pool", dst[:, :, 256:768], rows(win, 2048 + h * 512, 512), writes=[tok], key=key)
                toks.append(tok)
                ap, tok, kc, ncols = self.wb[f"rq_{i}_{h}"]
                dst = ap.rearrange("p (k n) -> p k n", k=8)
                S.dma("pool", dst[:, :, 0:256], rows(win, h * 256, 256), writes=[tok], key=key)
                S.dma("pool", dst[:, :, 256:768], rows(win, 4096 + h * 512, 512), writes=[tok], key=key)
                toks.append(tok)
                cast(f"rwo_{i}_{h}", self.r_w_out[i, h * 512:(h + 1) * 512, :].rearrange("(k p) n -> p k n", p=128), key)
                toks.append(self.wb[f"rwo_{i}_{h}"][1])
            for tk in toks:
                tk.lastw = (("d", key), S.dcount[key])
        for i in sorted(set(ev) | set(od)):
            if i in ev:
                cast_even(i)
            if i in od:
                cast_odd(i)
        if od:
            dc = VCOL["decay"]
            S.op("act", lambda e: e.activation(self.ldec.t[:], self.vecs.t[:, dc:dc + 16], AF.Sigmoid),
                 reads=[self.vecs.tok], writes=[self.ldec.tok])
            S.op("act", lambda e: e.activation(self.ldec.t[:], self.ldec.t[:], AF.Ln),
                 reads=[self.ldec.tok], writes=[self.ldec.tok])
            mm_ = A.mark()
            GW = self.GW
            dt_ = A.alloc("dtab", [128, GW], F32)
            rp = A.alloc("rp", [128, GW], F32)
            rn = A.alloc("rn", [128, GW], F32)
            S.dma("sp", dt_.t[:], self.dtab_in, writes=[dt_.tok], key="pro_ld_dt")
            S.op("dve", lambda e: e.tensor_scalar(rp.t[:], dt_.t[:], 0.0, None, ALU.max), reads=[dt_.tok], writes=[rp.tok])
            S.op("dve", lambda e: e.tensor_scalar(rn.t[:], dt_.t[:], -1.0, 0.0, ALU.mult, ALU.max), reads=[dt_.tok], writes=[rn.tok])
            e1 = A.alloc("e1", [128, GW], F32)
            e2 = A.alloc("e2", [128, GW], F32)
            gb = A.alloc("gb", [128, GW], BF16)
            for i in od:
                for h in range(4):
                    cf = i * 8 + h
                    cb = i * 8 + 4 + h
                    S.op("act", lambda e, cf=cf: e.activation(e1.t[:], rp.t[:], AF.Exp, scale=self.ldec.t[:, cf:cf + 1]),
                         reads=[rp.tok, self.ldec.tok], writes=[e1.tok])
                    S.op("act", lambda e, cb=cb: e.activation(e2.t[:], rn.t[:], AF.Exp, scale=self.ldec.t[:, cb:cb + 1]),
                         reads=[rn.tok, self.ldec.tok], writes=[e2.tok])
                    S.op("dve", lambda e: e.scalar_tensor_tensor(gb.t[:], e1.t[:], 0.0625, e2.t[:], ALU.mult, ALU.mult),
                         reads=[e1.tok, e2.tok], writes=[gb.tok])
                    S.dma("sp", self.gtab[i, h], gb.t[:], reads=[gb.tok], writes=[self.gtab_tok[i][h]], key="pro_st_g")
            A.release(mm_)
        A.release(m0)

    def sequence(self, s):
        A = self.A
        m0 = A.mark()
        if s == 0:
            for _ in self.load_x(s):
                pass
            A.release(m0)
        for n, l in enumerate(self.layers):
            if n + 1 < len(self.layers):
                nxt = self.layers[n + 1]
            elif s + 1 < self.nseq:
                nxt = self.layers[0]
            else:
                nxt = None
            if l % 2 == 0:
                self.even_layer(s, l // 2, nxt)
            else:
                self.odd_layer(s, l // 2, nxt)
        g1 = self.store_out(s)
        g2 = self.load_x(s + 1) if s + 1 < self.nseq else iter(())
        done1 = done2 = False
        while not (done1 and done2):
            if not done1:
                done1 = next(g1, "end") == "end"
            if not done2:
                done2 = next(g2, "end") == "end"
        A.release(m0)

    def load_x(self, s):
        S, A, T = self.S, self.A, self.T
        m0 = A.mark()
        stg = Ring([A.alloc(f"xs{i}", [128, 4, D], F32) for i in range(2)])
        ost = Ring([A.alloc(f"xo{i}", [128, 512], F32) for i in range(4)])
        for tg in range(self.NTG):
            sb = stg.next()
            src = self.x_in[s, tg * 512:(tg + 1) * 512, :].rearrange("(j p) d -> p j d", p=128)
            S.dma("sp", sb.t[:], src, reads=[self.x_in_tok], writes=[sb.tok], key=("xs", sb.off))
            for c in range(KC):
                ps = self.ps_gen.next()
                for j in range(4):
                    S.op("pe", lambda e, ps=ps, sb=sb, j=j, c=c: e.transpose(ps.t[:, j * 128:(j + 1) * 128], sb.t[:, j, c * 128:(c + 1) * 128], self.ident.t[:]),
                         reads=[sb.tok, self.ident.tok], writes=[ps.tok])
                ob = ost.next()
                eng = "act" if c % 2 == 0 else "dve"
                if eng == "act":
                    S.op("act", lambda e, ob=ob, ps=ps: e.copy(ob.t[:], ps.t[:]), reads=[ps.tok], writes=[ob.tok])
                else:
                    S.op("dve", lambda e, ob=ob, ps=ps: e.tensor_copy(ob.t[:], ps.t[:]), reads=[ps.tok], writes=[ob.tok])
                S.dma("sp", self.xT[s][c, :, tg * 512:(tg + 1) * 512], ob.t[:], reads=[ob.tok],
                      writes=[self.xT_tok[s][c][tg]], key=("xo", ob.off))
            yield tg

    def rope_tables(self, s, kind):
        S, A, T = self.S, self.A, self.T
        cs = A.alloc("cs", [128, T], F32)
        sn = A.alloc("sn", [128, T], F32)
        if kind == 64:
            self.cs64, self.sn64 = cs, sn
        else:
            self.cs256, self.sn256 = cs, sn
        m0 = A.mark()
        pi_ = A.alloc("pos_i", [128, T], I32)
        pf = A.alloc("pos_f", [128, T], F32)
        ang = A.alloc("ang", [128, T], F32)
        kf = A.alloc("kf", [128, T], F32)
        ki = A.alloc("ki", [128, T], I32)
        S.dma("sp", pi_.t[:], self.pos_in[s:s + 1, :].partition_broadcast(128), reads=[self.x_in_tok], writes=[pi_.tok], key="pos")
        S.op("dve", lambda e: e.tensor_copy(pf.t[:], pi_.t[:]), reads=[pi_.tok], writes=[pf.tok])
        TWO_PI = 2.0 * np.pi
        C1 = 6.28125
        C2 = TWO_PI - C1
        for (name, cs, sn) in ((f"inv{kind}", cs, sn),):
            inv = self.vcol(name)
            S.op("dve", lambda e, inv=inv: e.tensor_scalar(ang.t[:], pf.t[:], inv, None, ALU.mult), reads=[pf.tok, self.vecs.tok], writes=[ang.tok])
            S.op("dve", lambda e: e.tensor_scalar(kf.t[:], ang.t[:], 1.0 / TWO_PI, None, ALU.mult), reads=[ang.tok], writes=[kf.tok])
            S.op("dve", lambda e: e.tensor_copy(ki.t[:], kf.t[:]), reads=[kf.tok], writes=[ki.tok])
            S.op("dve", lambda e: e.tensor_copy(kf.t[:], ki.t[:]), reads=[ki.tok], writes=[kf.tok])
            S.op("dve", lambda e: e.scalar_tensor_tensor(ang.t[:], kf.t[:], -C1, ang.t[:], ALU.mult, ALU.add), reads=[kf.tok, ang.tok], writes=[ang.tok])
            S.op("dve", lambda e: e.scalar_tensor_tensor(ang.t[:], kf.t[:], -C2, ang.t[:], ALU.mult, ALU.add), reads=[kf.tok, ang.tok], writes=[ang.tok])
            S.op("dve", lambda e: e.tensor_scalar(ang.t[:], ang.t[:], np.pi, -np.pi, ALU.min, ALU.max), reads=[ang.tok], writes=[ang.tok])
            S.op("act", lambda e, sn=sn: e.activation(sn.t[:], ang.t[:], AF.Sin), reads=[ang.tok], writes=[sn.tok])
            S.op("dve", lambda e: e.tensor_scalar(kf.t[:], ang.t[:], np.pi / 2, np.pi, ALU.add, ALU.is_gt), reads=[ang.tok], writes=[kf.tok])
            S.op("dve", lambda e: e.scalar_tensor_tensor(ang.t[:], kf.t[:], -TWO_PI, ang.t[:], ALU.mult, ALU.add), reads=[kf.tok, ang.tok], writes=[ang.tok])
            S.op("dve", lambda e: e.tensor_scalar(ang.t[:], ang.t[:], np.pi / 2, np.pi, ALU.add, ALU.min), reads=[ang.tok], writes=[ang.tok])
            S.op("act", lambda e, cs=cs: e.activation(cs.t[:], ang.t[:], AF.Sin), reads=[ang.tok], writes=[cs.tok])
        A.release(m0)

    def make_h(self, s, gname, li):
        S, A, T = self.S, self.A, self.T
        hT = A.alloc("hT", [128, KC, T], BF16, ntok=KC * self.NTG)
        self.hT = hT
        htok = lambda c, g: hT.toks[c * self.NTG + g]
        self.htok = htok
        m0 = A.mark()
        xl = Ring([A.alloc(f"xl{i}", [128, KC, 512], F32) for i in range(2)])
        sq = Ring([A.alloc(f"sq{i}", [128, 512], BF16) for i in range(3)])
        tmp = Ring([A.alloc(f"lt{i}", [128, 512], F32) for i in range(1)])
        rs = Ring([A.alloc(f"rs{i}", [128, 512], F32) for i in range(1)])
        for tg in range(self.NTG):
            xb = xl.next()
            src = self.xT[s][:, :, tg * 512:(tg + 1) * 512].rearrange("c p t -> p c t")
            S.dma("sp", xb.t[:], src, reads=[self.xT_tok[s][c][tg] for c in range(KC)], writes=[xb.tok], key=("xl", xb.off))
            ps = self.ps_gen.next()
            for c in range(KC):
                q = sq.next()
                S.op("act", lambda e, q=q, xb=xb, c=c: e.activation(q.t[:], xb.t[:, c, :], AF.Square), reads=[xb.tok], writes=[q.tok])
                self.mm(ps.t[:], self.ones_d.t[:], q.t[:], c == 0, c == KC - 1, [self.ones_d.tok, q.tok], [ps.tok])
            t_ = tmp.next()
            r_ = rs.next()
            self.rstd_from_ps(ps, 1.0, t_, r_)
            for c in range(KC):
                eng = "dve"
                g = self.vcol(gname, li * 8 + c)
                S.op(eng, lambda e, xb=xb, c=c, g=g, r_=r_, tg=tg: e.scalar_tensor_tensor(
                    hT.t[:, c, tg * 512:(tg + 1) * 512], xb.t[:, c, :], g, r_.t[:], ALU.mult, ALU.mult),
                    reads=[xb.tok, r_.tok, self.vecs.tok], writes=[htok(c, tg)])
        A.release(m0)

    def acc_begin(self, s, tiles, ring, la=2):
        st = {"s": s, "tiles": tiles, "ring": ring, "la": la, "pend": [], "i": 0, "n": 0}
        for _ in range(min(la, len(tiles))):
            self._acc_issue(st)
        return st

    def _acc_issue(self, st):
        oc, tg = st["tiles"][st["n"]]
        st["n"] += 1
        s = st["s"]
        ob = st["ring"].next()
        tk = self.xT_tok[s][oc][tg]
        dst = self.xT[s][oc, :, tg * 512:(tg + 1) * 512]
        self.S.dma("sp", ob.t[:], dst, reads=[tk], writes=[ob.tok], key=("acc", ob.off))
        st["pend"].append((ob, tk, dst))

    def acc_step(self, st, ps):
        S = self.S
        if st["n"] < len(st["tiles"]):
            self._acc_issue(st)
        ob, tk, dst = st["pend"].pop(0)
        S.op("dve", lambda e: e.tensor_tensor(ob.t[:], ps.t[:], ob.t[:], ALU.add), reads=[ps.tok, ob.tok], writes=[ob.tok])
        S.dma("sp", dst, ob.t[:], reads=[ob.tok], writes=[tk], key=("acc", ob.off))

    def accum_x(self, s, oc, tg, ps, stg_ring, ev_ring=None):
        S = self.S
        ob = stg_ring.next()
        tk = self.xT_tok[s][oc][tg]
        dst = self.xT[s][oc, :, tg * 512:(tg + 1) * 512]
        S.dma("sp", ob.t[:], dst, reads=[tk], writes=[ob.tok], key=("acc", ob.off))
        if ev_ring is None:
            S.op("dve", lambda e: e.tensor_tensor(ob.t[:], ps.t[:], ob.t[:], ALU.add), reads=[ps.tok, ob.tok], writes=[ob.tok])
        else:
            ev = ev_ring.next()
            S.op("act", lambda e: e.copy(ev.t[:], ps.t[:]), reads=[ps.tok], writes=[ev.tok])
            S.op("pool", lambda e: e.tensor_tensor(ob.t[:], ev.t[:], ob.t[:], ALU.add), reads=[ev.tok, ob.tok], writes=[ob.tok])
        S.dma("sp", dst, ob.t[:], reads=[ob.tok], writes=[tk], key=("acc", ob.off))

    def even_layer(self, s, li, nxt=None):
        S, A, T, NTG, NKT = self.S, self.A, self.T, self.NTG, self.NKT
        m_layer = A.mark()
        self.rope_tables(s, 64)
        ym = A.alloc("ym", [128, 8, T], BF16, ntok=8 * NTG)
        ymtok = lambda c, g: ym.toks[c * NTG + g]
        accst = Ring([A.alloc(f"acs{i}", [128, 512], F32) for i in range(4)])
        cqn = A.alloc("cqn", [128, 3, T], BF16, ntok=3 * NTG)
        ckvn = A.alloc("ckvn", [128, T], BF16, ntok=NTG)
        krope = A.alloc("krope", [128, T], BF16, ntok=NTG)
        m_h = A.mark()
        self.make_h(s, "a_norm_g", li)
        hT, htok = self.hT, self.htok

        m1 = A.mark()
        PADL = 8
        TP = T + 16
        U = Ring([A.alloc(f"U{i}", [128, TP], F32) for i in range(1)])
        TA = A.alloc("TA", [128, TP], F32)
        TB = A.alloc("TB", [128, TP], F32)
        dT = A.alloc("dT", [128, 2, T], BF16, ntok=2)
        sg = Ring([A.alloc(f"sgb{i}", [128, 512], BF16) for i in range(2)])
        for g in range(4):
            w = POOL_W[g]
            right = w - 1 - w // 2
            wv, wtok = self.load_w(f"e1_{li}_{g}")
            pw_v, pw_tok = self.load_w(f"pw_{li}")
            for cc in range(2):
                u = U.next()
                S.op("pool", lambda e, u=u: e.memset(u.t[:, 0:PADL], 0.0), writes=[u.tok])
                S.op("pool", lambda e, u=u: e.memset(u.t[:, PADL + T:TP], 0.0), writes=[u.tok])
                for tg in range(NTG):
                    ps = self.ps_gen.next()
                    for kc in range(KC):
                        self.mm(ps.t[:], wv[:, kc, cc * 128:(cc + 1) * 128], hT.t[:, kc, tg * 512:(tg + 1) * 512],
                                kc == 0, kc == KC - 1, [wtok, htok(kc, tg)], [ps.tok])
                    S.op("act", lambda e, u=u, ps=ps, tg=tg: e.copy(u.t[:, PADL + tg * 512:PADL + (tg + 1) * 512], ps.t[:]),
                         reads=[ps.tok], writes=[u.tok])
                if s == 0:
                    self.dump(f"U_{g}_{cc}", u.t[:], [u.tok])
                cur = u
                span = 1
                bufs = [TA, TB]
                bi = 0
                lo = -PADL + 1
                while span * 2 < w:
                    dst = bufs[bi]
                    bi ^= 1
                    a0 = lo + PADL
                    n = TP - a0
                    S.op("dve", lambda e, dst=dst, cur=cur, a0=a0, n=n, span=span: e.tensor_tensor(
                        dst.t[:, a0:a0 + n], cur.t[:, a0:a0 + n], cur.t[:, a0 - span:a0 - span + n], ALU.add),
                        reads=[cur.tok], writes=[dst.tok])
                    cur = dst
                    span *= 2
                    lo += span
                dst = bufs[bi]
                S.op("dve", lambda e, dst=dst, cur=cur, right=right: e.tensor_tensor(
                    dst.t[:, PADL:PADL + T], cur.t[:, PADL - 1:PADL - 1 + T], cur.t[:, PADL + right:PADL + right + T], ALU.add),
                    reads=[cur.tok], writes=[dst.tok])
                S.op("dve", lambda e, dst=dst, u=u, cc=cc, w=w: e.scalar_tensor_tensor(
                    dT.t[:, cc, :], dst.t[:, PADL:PADL + T], 1.0 / w, u.t[:, PADL:PADL + T], ALU.mult, ALU.subtract),
                    reads=[dst.tok, u.tok], writes=[dT.toks[cc]])
                pc = VCOL["poolcnt"] + g * 16
                for (o0, c0) in ((0, pc), (T - 8, pc + 8)):
                    S.op("pool", lambda e, dst=dst, o0=o0, c0=c0: e.tensor_tensor(
                        dst.t[:, PADL + o0:PADL + o0 + 8], dst.t[:, PADL + o0:PADL + o0 + 8], self.vecs.t[:, c0:c0 + 8], ALU.mult),
                        reads=[dst.tok, self.vecs.tok], writes=[dst.tok])
                    S.op("pool", lambda e, dst=dst, u=u, o0=o0, cc=cc: e.tensor_tensor(
                        dT.t[:, cc, o0:o0 + 8], dst.t[:, PADL + o0:PADL + o0 + 8], u.t[:, PADL + o0:PADL + o0 + 8], ALU.subtract),
                        reads=[dst.tok, u.tok, dT.toks[cc]], writes=[dT.toks[cc]])
                ch = 2 * g + cc
                for tg in range(NTG):
                    psg = self.ps_gen.next()
                    for kc in range(KC):
                        self.mm(psg.t[:], wv[:, kc, 256 + cc * 128:256 + (cc + 1) * 128], hT.t[:, kc, tg * 512:(tg + 1) * 512],
                                kc == 0, kc == KC - 1, [wtok, htok(kc, tg)], [psg.tok])
                    S.op("act", lambda e, psg=psg, ch=ch, tg=tg: e.activation(ym.t[:, ch, tg * 512:(tg + 1) * 512], psg.t[:], AF.Silu),
                         reads=[psg.tok], writes=[ymtok(ch, tg)])
            if s == 0:
                self.dump(f"dT_{g}", dT.t[:], dT.toks)
            for j in range(2):
                ch = 2 * g + j
                for tg in range(NTG):
                    psy = self.ps_gen.next()
                    for cc in range(2):
                        self.mm(psy.t[:], pw_v[:, g * 2 + cc, j * 128:(j + 1) * 128], dT.t[:, cc, tg * 512:(tg + 1) * 512],
                                cc == 0, cc == 1, [pw_tok, dT.toks[cc]], [psy.tok])
                    sc = self.vcol("a_pool_scale", li * 8 + ch)
                    S.op("dve", lambda e, psy=psy, sc=sc, ch=ch, tg=tg: e.scalar_tensor_tensor(
                        ym.t[:, ch, tg * 512:(tg + 1) * 512], psy.t[:], sc, ym.t[:, ch, tg * 512:(tg + 1) * 512], ALU.mult, ALU.mult),
                        reads=[psy.tok, ymtok(ch, tg), self.vecs.tok], writes=[ymtok(ch, tg)])
        if s == 0:
            self.dump("ymb", ym.t[:], ym.toks)
            self.dump("hT", hT.t[:], hT.toks)
        A.release(m1)
        self.prefetch(f"wob0_{li}", f"wob1_{li}", f"e2_{li}", f"ga0_{li}")
        for hf in range(2):
            wv, wtok = self.load_w(f"wob{hf}_{li}")
            ast = self.acc_begin(s, [(hf * 4 + o4, tg) for tg in range(NTG) for o4 in range(4)], accst)
            for tg in range(NTG):
                for o4 in range(4):
                    ps = self.ps_gen.next()
                    for kc in range(8):
                        self.mm(ps.t[:], wv[:, kc, o4 * 128:(o4 + 1) * 128], ym.t[:, kc, tg * 512:(tg + 1) * 512],
                                kc == 0, kc == 7, [wtok, ymtok(kc, tg)], [ps.tok])
                    self.acc_step(ast, ps)
            self.prefetch(f"ga1_{li}" if hf == 0 else f"uq_{li}")
        m2 = A.mark()
        cqf = Ring([A.alloc(f"cqf{i}", [128, 4, 512], F32) for i in range(1)])
        sq = Ring([A.alloc(f"sq{i}", [128, 512], BF16) for i in range(4)])
        lt = Ring([A.alloc(f"lt{i}", [128, 512], F32) for i in range(2)])
        rsb = Ring([A.alloc(f"rs{i}", [128, 512], F32) for i in range(4)])
        rt = Ring([A.alloc(f"rt{i}", [128, 512], F32) for i in range(4)])
        wv, wtok = self.load_w(f"e2_{li}")
        for tg in range(NTG):
            tsl = slice(tg * 512, (tg + 1) * 512)
            cf = cqf.next()
            pss = self.ps_acc[0]
            psk = self.ps_acc[1]
            for j in range(4):
                ps = self.ps_gen.next()
                for kc in range(KC):
                    self.mm(ps.t[:], wv[:, kc, j * 128:(j + 1) * 128], hT.t[:, kc, tsl], kc == 0, kc == KC - 1,
                            [wtok, htok(kc, tg)], [ps.tok])
                S.op("dve", lambda e, cf=cf, ps=ps, j=j: e.tensor_copy(cf.t[:, j, :], ps.t[:]), reads=[ps.tok], writes=[cf.tok])
                q = sq.next()
                S.op("act", lambda e, q=q, cf=cf, j=j: e.activation(q.t[:], cf.t[:, j, :], AF.Square), reads=[cf.tok], writes=[q.tok])
                if j < 3:
                    self.mm(pss.t[:], self.ones.t[:], q.t[:], j == 0, j == 2, [self.ones.tok, q.tok], [pss.tok])
                else:
                    self.mm(psk.t[:], self.ones.t[:], q.t[:], True, True, [self.ones.tok, q.tok], [psk.tok])
            t1, r1 = lt.next(), rsb.next()
            self.rstd_from_ps(pss, 1.0 / 384.0, t1, r1)
            t2, r2 = lt.next(), rsb.next()
            self.rstd_from_ps(psk, 1.0 / 128.0, t2, r2)
            for j in range(3):
                g = self.vcol("a_q_norm_g", li * 3 + j)
                S.op("dve", lambda e, cf=cf, j=j, g=g, r1=r1, tsl=tsl: e.scalar_tensor_tensor(
                    cqn.t[:, j, tsl], cf.t[:, j, :], g, r1.t[:], ALU.mult, ALU.mult),
                    reads=[cf.tok, r1.tok, self.vecs.tok], writes=[cqn.toks[j * NTG + tg]])
            g = self.vcol("a_kv_norm_g", li)
            S.op("dve", lambda e, cf=cf, g=g, r2=r2, tsl=tsl: e.scalar_tensor_tensor(
                ckvn.t[:, tsl], cf.t[:, 3, :], g, r2.t[:], ALU.mult, ALU.mult),
                reads=[cf.tok, r2.tok, self.vecs.tok], writes=[ckvn.toks[tg]])
            psa = self.ps_gen.next()
            for kc in range(KC):
                self.mm(psa.t[0:64, :], wv[:, kc, 512:576], hT.t[:, kc, tsl], kc == 0, kc == KC - 1, [wtok, htok(kc, tg)], [psa.tok])
            psb = self.ps_gen.next()
            for kc in range(KC):
                self.mm(psb.t[0:64, :], wv[:, kc, 576:640], hT.t[:, kc, tsl], kc == 0, kc == KC - 1, [wtok, htok(kc, tg)], [psb.tok])
            ta, tb = rt.next(), rt.next()
            S.op("dve", lambda e, ta=ta, psa=psa, tsl=tsl: e.tensor_tensor(ta.t[0:64, :], psa.t[0:64, :], self.cs64.t[0:64, tsl], ALU.mult),
                 reads=[psa.tok, self.cs64.tok], writes=[ta.tok])
            S.op("dve", lambda e, tb=tb, psb=psb, tsl=tsl: e.tensor_tensor(tb.t[0:64, :], psb.t[0:64, :], self.sn64.t[0:64, tsl], ALU.mult),
                 reads=[psb.tok, self.sn64.tok], writes=[tb.tok])
            S.op("pool", lambda e, ta=ta, tb=tb, tsl=tsl: e.tensor_tensor(krope.t[0:64, tsl], ta.t[0:64, :], tb.t[0:64, :], ALU.add),
                 reads=[ta.tok, tb.tok], writes=[krope.toks[tg]])
        self.prefetch(f"ukv_{li}")
        for hf in range(2):
            wv, wtok = self.load_w(f"ga{hf}_{li}")
            for tg in range(NTG):
                tsl = slice(tg * 512, (tg + 1) * 512)
                for c4 in range(4):
                    ch = hf * 4 + c4
                    ps = self.ps_gen.next()
                    for kc in range(KC):
                        self.mm(ps.t[:], wv[:, kc, c4 * 128:(c4 + 1) * 128], hT.t[:, kc, tsl], kc == 0, kc == KC - 1,
                                [wtok, htok(kc, tg)], [ps.tok])
                    S.op("act", lambda e, ps=ps, ch=ch, tsl=tsl: e.activation(ym.t[:, ch, tsl], ps.t[:], AF.Silu),
                         reads=[ps.tok], writes=[ymtok(ch, tg)])
            self.prefetch(f"woa{hf}_{li}")
        A.release(m2)

        A.release(m_h)
        uq_v, uq_tok = self.load_w(f"uq_{li}")
        ukv_v, ukv_tok = self.load_w(f"ukv_{li}")
        KT = Ring([A.alloc(f"KT{i}", [128, T], BF16, ntok=NTG) for i in range(2)])
        VH = Ring([A.alloc(f"VH{i}", [128, NKT, 128], BF16, ntok=NTG) for i in range(2)])
        QN = Ring([A.alloc(f"QN{i}", [128, 512], BF16) for i in range(2)])
        QR = Ring([A.alloc(f"QR{i}", [128, 512], BF16) for i in range(2)])
        PT = Ring([A.alloc(f"PT{i}", [128, 512], BF16) for i in range(6)])
        rt = Ring([A.alloc(f"rt{i}", [128, 512], F32) for i in range(4)])
        rcp = Ring([A.alloc(f"rcp{i}", [128, 512], F32) for i in range(2)])
        accP = Ring([A.alloc(f"accP{i}", [128, 512], F32) for i in range(2)])
        accD = Ring([A.alloc(f"accD{i}", [128, 512], F32) for i in range(2)])
        self.phaseB_top = A.cur
        scale = float(192 ** -0.5)
        def kv_proj(h):
            kt_b = KT.next()
            vh_b = VH.next()
            for tg in range(NTG):
                tsl = slice(tg * 512, (tg + 1) * 512)
                ps = self.ps_gen.next()
                self.mm(ps.t[:], ukv_v[:, 0, h * 256:h * 256 + 128], ckvn.t[:, tsl], True, True, [ukv_tok, ckvn.toks[tg]], [ps.tok])
                S.op("dve", lambda e: e.tensor_copy(kt_b.t[:, tsl], ps.t[:]), reads=[ps.tok], writes=[kt_b.toks[tg]])
                ps = self.ps_gen.next()
                for j in range(4):
                    kt = tg * 4 + j
                    self.mm(ps.t[:, j * 128:(j + 1) * 128], ckvn.t[:, kt * 128:(kt + 1) * 128], ukv_v[:, 0, h * 256 + 128:h * 256 + 256],
                            True, True, [ukv_tok, ckvn.toks[tg]], [ps.tok])
                S.op("act", lambda e: e.copy(
                    vh_b.t[:, tg * 4:(tg + 1) * 4, :].rearrange("p a b -> p (a b)"), ps.t[:]), reads=[ps.tok], writes=[vh_b.toks[tg]])
            return kt_b, vh_b

        def q_proj(h, qg):
            qsl = slice(qg * 512, (qg + 1) * 512)
            qn, qr = QN.next(), QR.next()
            ps = self.ps_gen.next()
            for k3 in range(3):
                self.mm(ps.t[:], uq_v[:, k3, h * 256:h * 256 + 128], cqn.t[:, k3, qsl], k3 == 0, k3 == 2,
                        [uq_tok, cqn.toks[k3 * NTG + qg]], [ps.tok])
            S.op("act", lambda e: e.copy(qn.t[:], ps.t[:]), reads=[ps.tok], writes=[qn.tok])
            psa = self.ps_gen.next()
            for k3 in range(3):
                self.mm(psa.t[0:64, :], uq_v[:, k3, h * 256 + 128:h * 256 + 192], cqn.t[:, k3, qsl], k3 == 0, k3 == 2,
                        [uq_tok, cqn.toks[k3 * NTG + qg]], [psa.tok])
            psb = self.ps_gen.next()
            for k3 in range(3):
                self.mm(psb.t[0:64, :], uq_v[:, k3, h * 256 + 192:h * 256 + 256], cqn.t[:, k3, qsl], k3 == 0, k3 == 2,
                        [uq_tok, cqn.toks[k3 * NTG + qg]], [psb.tok])
            ta, tb = rt.next(), rt.next()
            S.op("dve", lambda e: e.tensor_tensor(ta.t[0:64, :], psa.t[0:64, :], self.cs64.t[0:64, qsl], ALU.mult),
                 reads=[psa.tok, self.cs64.tok], writes=[ta.tok])
            S.op("dve", lambda e: e.tensor_tensor(tb.t[0:64, :], psb.t[0:64, :], self.sn64.t[0:64, qsl], ALU.mult),
                 reads=[psb.tok, self.sn64.tok], writes=[tb.tok])
            S.op("pool", lambda e: e.tensor_tensor(qr.t[0:64, :], ta.t[0:64, :], tb.t[0:64, :], ALU.add),
                 reads=[ta.tok, tb.tok], writes=[qr.tok])
            return qn, qr

        def attend(h, qg, kt_b, vh_b, qn, qr, acc_o, acc_s, prev_fin):
            qsl = slice(qg * 512, (qg + 1) * 512)
            ap_, ad_ = accP.next(), accD.next()

            def score(kt):
                ps = self.ps_gen.next()
                ksl = slice(kt * 128, (kt + 1) * 128)
                self.mm(ps.t[:], kt_b.t[:, ksl], qn.t[:], True, False, [kt_b.toks[kt // 4], qn.tok], [ps.tok])
                self.mm(ps.t[:], krope.t[0:64, ksl], qr.t[0:64, :], False, True, [krope.toks[kt // 4], qr.tok], [ps.tok])
                p = PT.next()
                S.op("act", lambda e: e.activation(p.t[:], ps.t[:], AF.Exp, scale=scale), reads=[ps.tok], writes=[p.tok])
                return p

            def pv(kt, p):
                self.mm(acc_o.t[:], vh_b.t[:, kt, :], p.t[:], kt == 0, kt == NKT - 1, [vh_b.toks[kt // 4], p.tok], [acc_o.tok])
                if kt % 4 == 3:
                    self.mm(acc_s.t[:], self.ones.t[:], p.t[:], kt == 3, False, [self.ones.tok, p.tok], [acc_s.tok])
                elif kt == 0:
                    S.op("dve", lambda e: e.tensor_copy(ad_.t[:], p.t[:]), reads=[p.tok], writes=[ad_.tok])
                else:
                    S.op("dve", lambda e: e.tensor_tensor(ad_.t[:], ad_.t[:], p.t[:], ALU.add), reads=[ad_.tok, p.tok], writes=[ad_.tok])

            LA = 3
            pend = [score(k) for k in range(min(LA, NKT))]
            for kt in range(NKT):
                if kt + LA < NKT:
                    pend.append(score(kt + LA))
                pv(kt, pend.pop(0))
                if kt == 1 and prev_fin is not None:
                    prev_fin()

            def fin():
                self.mm(acc_s.t[:], self.ones_f.t[:], ad_.t[:], NKT < 4, True, [self.ones_f.tok, ad_.tok], [acc_s.tok])
                rc = rcp.next()
                S.op("dve", lambda e: e.reciprocal(rc.t[:], acc_s.t[:]), reads=[acc_s.tok], writes=[rc.tok])
                tn = rt.next()
                S.op("dve", lambda e: e.tensor_tensor(tn.t[:], acc_o.t[:], rc.t[:], ALU.mult),
                     reads=[acc_o.tok, rc.tok], writes=[tn.tok])
                S.op("pool", lambda e: e.tensor_tensor(ym.t[:, h, qsl], tn.t[:], ym.t[:, h, qsl], ALU.mult),
                     reads=[tn.tok, ymtok(h, qg)], writes=[ymtok(h, qg)])
            return fin

        units = [(h, qg) for h in range(8) for qg in range(NTG)]
        kvs = {0: kv_proj(0)}
        qs = {units[0]: q_proj(*units[0])}
        fin = None
        for i, (h, qg) in enumerate(units):
            if i + 1 < len(units):
                nh, nq = units[i + 1]
                if nh not in kvs:
                    kvs[nh] = kv_proj(nh)
                qs[(nh, nq)] = q_proj(nh, nq)
            kt_b, vh_b = kvs[h]
            qn, qr = qs.pop((h, qg))
            fin = attend(h, qg, kt_b, vh_b, qn, qr, self.ps_acc[(i % 2) * 2], self.ps_acc[(i % 2) * 2 + 1], fin)
        fin()
        nb = self.next_blocks(nxt)
        self.prefetch(nb[0], nb[1])
        for hf in range(2):
            wv, wtok = self.load_w(f"woa{hf}_{li}")
            ast = self.acc_begin(s, [(hf * 4 + o4, tg) for tg in range(NTG) for o4 in range(4)], accst)
            for tg in range(NTG):
                for o4 in range(4):
                    ps = self.ps_gen.next()
                    for kc in range(8):
                        self.mm(ps.t[:], wv[:, kc, o4 * 128:(o4 + 1) * 128], ym.t[:, kc, tg * 512:(tg + 1) * 512],
                                kc == 0, kc == 7, [wtok, ymtok(kc, tg)], [ps.tok])
                    self.acc_step(ast, ps)
            if hf == 0:
                self.prefetch(nb[2])
        A.release(m_layer)

    def odd_layer(self, s, li, nxt=None):
        S, A, T, NTG, NKT = self.S, self.A, self.T, self.NTG, self.NKT
        m_layer = A.mark()
        self.make_h(s, "r_norm_g", li)
        self.rope_tables(s, 256)
        hT, htok = self.hT, self.htok
        accst = Ring([A.alloc(f"acs{i}", [128, 512], F32) for i in range(4)])
        KTb = A.alloc("rKT", [128, 2, T], BF16, ntok=2 * NTG)
        VHb = A.alloc("rVH", [128, NKT, 512], BF16, ntok=NKT)
        Gr = Ring([A.alloc(f"rG{i}", [128, self.GW], BF16) for i in range(2)])
        gbs = {}

        def load_g(h):
            if h in gbs or h > 3:
                return
            gb = Gr.next()
            S.dma("sp", gb.t[:], self.gtab[li, h], reads=[self.gtab_tok[li][h]], writes=[gb.tok], key=("gld", gb.off))
            gbs[h] = gb
        QT = Ring([A.alloc(f"rQT{i}", [128, 2, 512], BF16, ntok=2) for i in range(2)])
        SG = Ring([A.alloc(f"rSG{i}", [128, 4, 512], BF16, ntok=4) for i in range(2)])
        PT = Ring([A.alloc(f"rPT{i}", [128, 512], BF16) for i in range(4)])
        rt = Ring([A.alloc(f"rrt{i}", [128, 512], F32) for i in range(6)])
        obf = A.alloc("obf", [128, 4, 512], BF16, ntok=4)
        osq = A.alloc("osq", [128, 4, 512], BF16, ntok=4)
        st = [A.alloc(f"gst{i}", [128, 512], F32) for i in range(5)]
        ymr = Ring([A.alloc(f"rym{i}", [128, 4, 512], BF16, ntok=4) for i in range(1)])

        def rope_pair(ps1, ps2, sl, out1, out2, o1tok, o2tok):
            a, b, c, d = rt.next(), rt.next(), rt.next(), rt.next()
            S.op("dve", lambda e: e.tensor_tensor(a.t[:], ps1.t[:], self.cs256.t[:, sl], ALU.mult), reads=[ps1.tok, self.cs256.tok], writes=[a.tok])
            S.op("dve", lambda e: e.tensor_tensor(b.t[:], ps2.t[:], self.sn256.t[:, sl], ALU.mult), reads=[ps2.tok, self.sn256.tok], writes=[b.tok])
            S.op("dve", lambda e: e.tensor_tensor(c.t[:], ps2.t[:], self.cs256.t[:, sl], ALU.mult), reads=[ps2.tok, self.cs256.tok], writes=[c.tok])
            S.op("dve", lambda e: e.tensor_tensor(d.t[:], ps1.t[:], self.sn256.t[:, sl], ALU.mult), reads=[ps1.tok, self.sn256.tok], writes=[d.tok])
            S.op("dve", lambda e: e.tensor_tensor(out1, a.t[:], b.t[:], ALU.subtract), reads=[a.tok, b.tok], writes=[o1tok])
            S.op("dve", lambda e: e.tensor_tensor(out2, c.t[:], d.t[:], ALU.add), reads=[c.tok, d.tok], writes=[o2tok])

        wts = {}

        def pre(h):
            load_g(h)
            wv, wtok = self.load_w(f"rk_{li}_{h}")
            for tg in range(NTG):
                tsl = slice(tg * 512, (tg + 1) * 512)
                pp = []
                for j in range(2):
                    ps = self.ps_gen.next()
                    for kc in range(KC):
                        self.mm(ps.t[:], wv[:, kc, j * 128:(j + 1) * 128], hT.t[:, kc, tsl], kc == 0, kc == KC - 1, [wtok, htok(kc, tg)], [ps.tok])
                    pp.append(ps)
                rope_pair(pp[0], pp[1], tsl, KTb.t[:, 0, tsl], KTb.t[:, 1, tsl], KTb.toks[tg], KTb.toks[NTG + tg])
            for kt in range(NKT):
                ps = self.ps_gen.next()
                for kc in range(KC):
                    self.mm(ps.t[:], hT.t[:, kc, kt * 128:(kt + 1) * 128], wv[:, kc, 256:768], kc == 0, kc == KC - 1, [wtok, htok(kc, kt // 4)], [ps.tok])
                S.op("act", lambda e: e.copy(VHb.t[:, kt, :], ps.t[:]), reads=[ps.tok], writes=[VHb.toks[kt]])
            wts[h] = (self.load_w(f"rq_{li}_{h}"), self.load_w(f"rwo_{li}_{h}"))

        def stage_a(h, qg):
            (qv, qtok), _ = wts[h]
            qsl = slice(qg * 512, (qg + 1) * 512)
            qt = QT.next()
            pp = []
            for j in range(2):
                ps = self.ps_gen.next()
                for kc in range(KC):
                    self.mm(ps.t[:], qv[:, kc, j * 128:(j + 1) * 128], hT.t[:, kc, qsl], kc == 0, kc == KC - 1, [qtok, htok(kc, qg)], [ps.tok])
                pp.append(ps)
            rope_pair(pp[0], pp[1], qsl, qt.t[:, 0, :], qt.t[:, 1, :], qt.toks[0], qt.toks[1])
            sgb = SG.next()
            for j in range(4):
                ps = self.ps_gen.next()
                for kc in range(KC):
                    self.mm(ps.t[:], qv[:, kc, 256 + j * 128:256 + (j + 1) * 128], hT.t[:, kc, qsl], kc == 0, kc == KC - 1, [qtok, htok(kc, qg)], [ps.tok])
                S.op("act", lambda e: e.activation(sgb.t[:, j, :], ps.t[:], AF.Silu), reads=[ps.tok], writes=[sgb.toks[j]])
                g = self.vcol("r_gn_g", li * 16 + h * 4 + j)
                S.op("dve", lambda e: e.tensor_scalar(sgb.t[:, j, :], sgb.t[:, j, :], g, None, ALU.mult),
                     reads=[sgb.toks[j], self.vecs.tok], writes=[sgb.toks[j]])
            return qt, sgb

        def stage_b(h, qg, qt):
            Gb = gbs[h]

            def score(kt):
                ps = self.ps_gen.next()
                ksl = slice(kt * 128, (kt + 1) * 128)
                for j in range(2):
                    self.mm(ps.t[:], KTb.t[:, j, ksl], qt.t[:, j, :], j == 0, j == 1, [KTb.toks[j * NTG + kt // 4], qt.toks[j]], [ps.tok])
                p = PT.next()
                c0 = qg * 512 - kt * 128 + T - 128
                S.op("dve", lambda e: e.tensor_tensor(p.t[:], ps.t[:], Gb.t[:, c0:c0 + 512], ALU.mult),
                     reads=[ps.tok, Gb.tok], writes=[p.tok])
                return p

            def pv(kt, p):
                for jv in range(4):
                    acc = self.ps_acc[jv]
                    self.mm(acc.t[:], VHb.t[:, kt, jv * 128:(jv + 1) * 128], p.t[:], kt == 0, kt == NKT - 1, [VHb.toks[kt], p.tok], [acc.tok])

            LA = 3
            pend = [score(k) for k in range(min(LA, NKT))]
            for kt in range(NKT):
                if kt + LA < NKT:
                    pend.append(score(kt + LA))
                pv(kt, pend.pop(0))
            for jv in range(4):
                acc = self.ps_acc[jv]
                S.op("act", lambda e: e.copy(obf.t[:, jv, :], acc.t[:]), reads=[acc.tok], writes=[obf.toks[jv]])
                S.op("act", lambda e: e.activation(osq.t[:, jv, :], obf.t[:, jv, :], AF.Square), reads=[obf.toks[jv]], writes=[osq.toks[jv]])

        def stage_d(h, qg, sgb):
            ps1 = self.ps_gen.next()
            for jv in range(4):
                self.mm(ps1.t[:], self.ones_v.t[:], obf.t[:, jv, :], jv == 0, jv == 3, [self.ones_v.tok, obf.toks[jv]], [ps1.tok])
            ps2 = self.ps_gen.next()
            for jv in range(4):
                self.mm(ps2.t[:], self.ones_v.t[:], osq.t[:, jv, :], jv == 0, jv == 3, [self.ones_v.tok, osq.toks[jv]], [ps2.tok])
            mean, msq, var, lt_, rstd = st
            S.op("act", lambda e: e.copy(mean.t[:], ps1.t[:]), reads=[ps1.tok], writes=[mean.tok])
            S.op("act", lambda e: e.activation(msq.t[:], mean.t[:], AF.Square), reads=[mean.tok], writes=[msq.tok])
            S.op("dve", lambda e: e.tensor_tensor(var.t[:], ps2.t[:], msq.t[:], ALU.subtract), reads=[ps2.tok, msq.tok], writes=[var.tok])
            S.op("dve", lambda e: e.tensor_scalar(var.t[:], var.t[:], 0.0, None, ALU.max), reads=[var.tok], writes=[var.tok])
            S.op("act", lambda e: e.activation(lt_.t[:], var.t[:], AF.Ln, bias=EPS, scale=1.0), reads=[var.tok], writes=[lt_.tok])
            S.op("act", lambda e: e.activation(rstd.t[:], lt_.t[:], AF.Exp, scale=-0.5), reads=[lt_.tok], writes=[rstd.tok])
            ymb = ymr.next()
            for jv in range(4):
                a, b = rt.next(), rt.next()
                S.op("pool", lambda e: e.tensor_tensor(a.t[:], obf.t[:, jv, :], mean.t[:], ALU.subtract),
                     reads=[obf.toks[jv], mean.tok], writes=[a.tok])
                S.op("pool", lambda e: e.tensor_tensor(b.t[:], a.t[:], rstd.t[:], ALU.mult),
                     reads=[a.tok, rstd.tok], writes=[b.tok])
                S.op("pool", lambda e: e.tensor_tensor(ymb.t[:, jv, :], b.t[:], sgb.t[:, jv, :], ALU.mult),
                     reads=[b.tok, sgb.toks[jv]], writes=[ymb.toks[jv]])
            return ymb

        def stage_f(h, qg, ymb, ast):
            _, (ov, otok) = wts[h]
            for oc in range(8):
                ps = self.ps_gen.next()
                for kc in range(4):
                    self.mm(ps.t[:], ov[:, kc, oc * 128:(oc + 1) * 128], ymb.t[:, kc, :], kc == 0, kc == 3, [otok, ymb.toks[kc]], [ps.tok])
                self.acc_step(ast, ps)

        units = [(h, qg) for h in range(4) for qg in range(NTG)]
        prev = None
        nb = self.next_blocks(nxt)

        def upcoming(h):
            if h < 3:
                return (f"rk_{li}_{h + 1}", f"rq_{li}_{h + 1}", f"rwo_{li}_{h + 1}")
            return nb

        def acc_tiles(qg):
            return self.acc_begin(s, [(oc, qg) for oc in range(8)], accst)

        for i, (h, qg) in enumerate(units):
            if qg == 0:
                pre(h)
            qt, sgb = stage_a(h, qg)
            if qg == NTG - 1 and NTG >= 2:
                self.prefetch(upcoming(h)[2])
            if prev is not None:
                prev["ymb"] = stage_d(prev["h"], prev["qg"], prev["sgb"])
                prev["ast"] = acc_tiles(prev["qg"])
            stage_b(h, qg, qt)
            if prev is not None:
                stage_f(prev["h"], prev["qg"], prev["ymb"], prev["ast"])
            if qg == min(1, NTG - 2) and NTG >= 2:
                up = upcoming(h)
                self.prefetch(up[0], up[1])
                load_g(h + 1)
            prev = {"h": h, "qg": qg, "sgb": sgb}
        prev["ymb"] = stage_d(prev["h"], prev["qg"], prev["sgb"])
        prev["ast"] = acc_tiles(prev["qg"])
        stage_f(prev["h"], prev["qg"], prev["ymb"], prev["ast"])
        A.release(m_layer)

    def store_out(self, s):
        S, A, T = self.S, self.A, self.T
        m0 = A.mark()
        xl = Ring([A.alloc(f"fxl{i}", [128, KC, 512], F32) for i in range(2)])
        sq = Ring([A.alloc(f"fsq{i}", [128, 512], BF16) for i in range(3)])
        tmp = Ring([A.alloc(f"flt{i}", [128, 512], F32) for i in range(2)])
        rs = Ring([A.alloc(f"frs{i}", [128, 512], F32) for i in range(2)])
        yb = Ring([A.alloc(f"fy{i}", [128, KC, 512], F32) for i in range(2)])
        ost = Ring([A.alloc(f"fo{i}", [128, D], F32) for i in range(3)])
        for tg in range(self.NTG):
            xb = xl.next()
            src = self.xT[s][:, :, tg * 512:(tg + 1) * 512].rearrange("c p t -> p c t")
            S.dma("sp", xb.t[:], src, reads=[self.xT_tok[s][c][tg] for c in range(KC)], writes=[xb.tok], key=("fxl", xb.off))
            if self.final_norm:
                ps = self.ps_gen.next()
                for c in range(KC):
                    q = sq.next()
                    S.op("act", lambda e, q=q, xb=xb, c=c: e.activation(q.t[:], xb.t[:, c, :], AF.Square), reads=[xb.tok], writes=[q.tok])
                    self.mm(ps.t[:], self.ones_d.t[:], q.t[:], c == 0, c == KC - 1, [self.ones_d.tok, q.tok], [ps.tok])
                t_, r_ = tmp.next(), rs.next()
                self.rstd_from_ps(ps, 1.0, t_, r_)
                y = yb.next()
                for c in range(KC):
                    g = self.vcol("final_g", c)
                    S.op("dve", lambda e, xb=xb, c=c, g=g, r_=r_, y=y: e.scalar_tensor_tensor(
                        y.t[:, c, :], xb.t[:, c, :], g, r_.t[:], ALU.mult, ALU.mult),
                        reads=[xb.tok, r_.tok, self.vecs.tok], writes=[y.tok])
            else:
                y = xb
            for j in range(4):
                ob = ost.next()
                for half in range(2):
                    ps = self.ps_gen.next()
                    for cc in range(4):
                        c = half * 4 + cc
                        S.op("pe", lambda e, ps=ps, y=y, j=j, c=c, cc=cc: e.transpose(ps.t[:, cc * 128:(cc + 1) * 128], y.t[:, c, j * 128:(j + 1) * 128], self.ident.t[:]),
                             reads=[y.tok, self.ident.tok], writes=[ps.tok])
                    if half == 0:
                        S.op("act", lambda e, ob=ob, ps=ps: e.copy(ob.t[:, 0:512], ps.t[:]), reads=[ps.tok], writes=[ob.tok])
                    else:
                        S.op("dve", lambda e, ob=ob, ps=ps: e.tensor_copy(ob.t[:, 512:1024], ps.t[:]), reads=[ps.tok, ob.tok], writes=[ob.tok])
                t0 = tg * 512 + j * 128
                S.dma("sp", self.out[s, t0:t0 + 128, :], ob.t[:], reads=[ob.tok], writes=[self.out_tok], key=("fo", ob.off))
            yield tg


_PROG_CACHE = {}


def _get_prog(T, nseq, layers, final_norm, debug=False):
    k = (T, nseq, tuple(layers), final_norm, debug)
    if k not in _PROG_CACHE:
        _PROG_CACHE[k] = Prog(T, nseq, list(layers), final_norm, debug=debug)
    return _PROG_CACHE[k]


def _dtab(T):
    GW = 2 * T - 128
    c = np.arange(GW, dtype=np.float32)[None, :]
    i = np.arange(128, dtype=np.float32)[:, None]
    return np.ascontiguousarray(c - i - np.float32(T - 128)).astype(np.float32)


def run_layers(inp, layers, final_norm, x=None, n_cores=N_CORES, debug=False):
    x = np.asarray(inp["x"] if x is None else x, np.float32)
    B, T, _ = x.shape
    nseq = B // n_cores
    prog = _get_prog(T, nseq, layers, final_norm, debug)
    vecs = _build_vecs(inp, T)
    ident = np.eye(128, dtype=np.float32)
    dtab = _dtab(T)
    pos = np.asarray(inp["positions"], np.int32)
    shared = {k: np.ascontiguousarray(np.asarray(inp[k], np.float32)) for k in
              ("a_w_in", "a_w_uq", "a_w_ukv", "a_pool_w", "a_w_out", "r_w_in", "r_w_out")}
    in_maps = []
    for c in range(n_cores):
        m = dict(shared)
        m["x"] = np.ascontiguousarray(x[c * nseq:(c + 1) * nseq])
        m["pos"] = np.ascontiguousarray(pos[c * nseq:(c + 1) * nseq])
        m["vecs"] = vecs
        m["ident"] = ident
        m["dtab"] = dtab
        in_maps.append(m)
    res = run_bass_kernel_spmd(prog.nc, in_maps, core_ids=list(range(n_cores)))
    if debug:
        return res.results
    return np.concatenate([r["out"] for r in res.results], axis=0)


FUSED = True


def kernel(**inputs):
    if FUSED:
        return run_layers(inputs, [0, 1, 2, 3], True)
    x = None
    for l in range(4):
        x = run_layers(inputs, [l], l == 3, x=x)
    return x
```
